# Optimizing a Trainium2 kernel written in Bass

```python
import jax, jax.numpy as jnp
from jax import lax
import numpy as np

D_MODEL = 1024
BATCH = 2
SEQ = 8192
DEPTH = 1

CTX_LEN = 256
GRID_W = 64
ATT_HEADS = 8
ATT_KV_HEADS = 2
ATT_HEAD_DIM = 64
GLA_HEADS = 4
GLA_KEY_DIM = 64
GLA_VAL_DIM = 128
GLA_GATE_RANK = 16
GLA_GATE_NORMALIZER = 16.0
GLA_CHUNK = 64
D_FF = 2816
CONV_WIDTH = 3
Q_BLOCK = 128
ROPE_THETA = 10000.0
EPS = 1e-6

ATT_Q_W = ATT_HEADS * ATT_HEAD_DIM
ATT_KV_W = ATT_KV_HEADS * ATT_HEAD_DIM
GLA_QK_W = GLA_HEADS * GLA_KEY_DIM
GLA_V_W = GLA_HEADS * GLA_VAL_DIM
MIX_W = ATT_Q_W + GLA_V_W
IN_OFFSETS = (512, 640, 768, 1024, 1280, 1792, 1824)
IN_W = 2336

kernel_name = "hymba_style_attn_gla_convffn_dit_layer"


def rms_norm(x, g):
    xf = x.astype(jnp.float32)
    y = xf * lax.rsqrt(jnp.mean(xf * xf, axis=-1, keepdims=True) + EPS)
    return (y * g.astype(jnp.float32)).astype(x.dtype)


def modulate(h, shift, scale):
    return h * (1 + scale) + shift


def adaln(cond, w, b):
    m = jax.nn.silu(cond) @ w + b
    return jnp.split(m, 6, axis=-1)


def axial_rope_tables(rows):
    row = jnp.broadcast_to(jnp.arange(rows, dtype=jnp.float32)[:, None], (rows, GRID_W)).reshape(-1)
    col = jnp.broadcast_to(jnp.arange(GRID_W, dtype=jnp.float32)[None, :], (rows, GRID_W)).reshape(-1)
    half = ATT_HEAD_DIM // 2
    freqs = ROPE_THETA ** (-jnp.arange(0, half, 2, dtype=jnp.float32) / half)
    ang = jnp.concatenate([row[:, None] * freqs, col[:, None] * freqs], axis=-1)
    return jnp.cos(ang), jnp.sin(ang)


def apply_rope(x, cos, sin):
    xp = x.reshape(x.shape[:-1] + (x.shape[-1] // 2, 2))
    x0, x1 = xp[..., 0], xp[..., 1]
    c = cos[:, None, :].astype(x.dtype)
    s = sin[:, None, :].astype(x.dtype)
    return jnp.stack([x0 * c - x1 * s, x0 * s + x1 * c], axis=-1).reshape(x.shape)


def attend(qg, k, v):
    s = jnp.einsum('bqkgd,bskd->bkgqs', qg, k, preferred_element_type=jnp.float32) * (ATT_HEAD_DIM ** -0.5)
    p = jax.nn.softmax(s, axis=-1)
    return jnp.einsum('bkgqs,bskd->bqkgd', p.astype(v.dtype), v)


def latent_attention(q, k_lat, v_lat, k_ctx, v_ctx):
    B, S = q.shape[0], q.shape[1]
    G = ATT_HEADS // ATT_KV_HEADS
    k_all = jnp.concatenate([k_ctx, k_lat], axis=1)
    v_all = jnp.concatenate([v_ctx, v_lat], axis=1)
    nb = S // Q_BLOCK
    qb = q.reshape(B, nb, Q_BLOCK, ATT_KV_HEADS, G, ATT_HEAD_DIM).transpose(1, 0, 2, 3, 4, 5)
    o = lax.map(lambda qi: attend(qi, k_all, v_all), qb)
    return o.transpose(1, 0, 2, 3, 4, 5).reshape(B, S, ATT_Q_W)


def context_attention(q, k, v):
    B, L = q.shape[0], q.shape[1]
    G = ATT_HEADS // ATT_KV_HEADS
    o = attend(q.reshape(B, L, ATT_KV_HEADS, G, ATT_HEAD_DIM), k, v)
    return o.reshape(B, L, ATT_Q_W)


def gla_chunked(q, k, v, log_a, state0):
    B, L, H, dk = q.shape
    dv = v.shape[-1]
    n = L // GLA_CHUNK

    def chunks(t):
        return t.astype(jnp.float32).reshape(B, n, GLA_CHUNK, H, t.shape[-1]).transpose(0, 3, 1, 2, 4)

    qc, kc, vc, gc = chunks(q), chunks(k), chunks(v), chunks(log_a)
    qc = qc * (dk ** -0.5)
    b = lax.cumsum(gc, axis=3)
    b_last = b[:, :, :, -1:, :]
    q_in = qc * jnp.exp(b)
    k_in = kc * jnp.exp(-b)
    k_up = kc * jnp.exp(b_last - b)
    mask = jnp.tril(jnp.ones((GLA_CHUNK, GLA_CHUNK), dtype=bool))
    att = jnp.where(mask, jnp.einsum('bhntd,bhnsd->bhnts', q_in, k_in), 0.0)
    o_intra = jnp.einsum('bhnts,bhnsv->bhntv', att, vc)
    upd = jnp.einsum('bhnsd,bhnsv->bhndv', k_up, vc)
    decay = jnp.exp(b_last[:, :, :, 0, :])

    def step(S, inp):
        dec, u = inp
        return dec[..., None] * S + u, S

    s_final, s_starts = lax.scan(step, state0.astype(jnp.float32),
                                 (decay.transpose(2, 0, 1, 3), upd.transpose(2, 0, 1, 3, 4)))
    s_starts = s_starts.transpose(1, 2, 0, 3, 4)
    o = o_intra + jnp.einsum('bhntd,bhndv->bhntv', q_in, s_starts)
    return o.transpose(0, 2, 3, 1, 4).reshape(B, L, H, dv), s_final


def log_gate(gl, w2, b2):
    B, L = gl.shape[0], gl.shape[1]
    z = (gl @ w2 + b2).astype(jnp.float32)
    return (jax.nn.log_sigmoid(z) / GLA_GATE_NORMALIZER).reshape(B, L, GLA_HEADS, GLA_KEY_DIM)


def gla_bidirectional(q, k, v, log_f, log_b, s0_f, s0_b):
    flip = lambda t: t[:, ::-1]
    o_f, s_f = gla_chunked(q, k, v, log_f, s0_f)
    o_b, s_b = gla_chunked(flip(q), flip(k), flip(v), flip(log_b), s0_b)
    return o_f + flip(o_b), s_f, s_b


def mixer_inputs(h, w_in_l, g_q_l, g_k_l, w_gf, b_gf, w_gb, b_gb):
    B, L = h.shape[0], h.shape[1]
    z = h @ w_in_l
    aq, ak, av, gq, gk, gv, gl, gr = jnp.split(z, IN_OFFSETS, axis=-1)
    aq = rms_norm(aq.reshape(B, L, ATT_HEADS, ATT_HEAD_DIM), g_q_l)
    ak = rms_norm(ak.reshape(B, L, ATT_KV_HEADS, ATT_HEAD_DIM), g_k_l)
    av = av.reshape(B, L, ATT_KV_HEADS, ATT_HEAD_DIM)
    gq = gq.reshape(B, L, GLA_HEADS, GLA_KEY_DIM)
    gk = gk.reshape(B, L, GLA_HEADS, GLA_KEY_DIM)
    gv = gv.reshape(B, L, GLA_HEADS, GLA_VAL_DIM)
    log_f = log_gate(gl[..., :GLA_GATE_RANK], w_gf, b_gf)
    log_b = log_gate(gl[..., GLA_GATE_RANK:], w_gb, b_gb)
    return aq, ak, av, gq, gk, gv, log_f, log_b, gr


def merge_heads(att_o, gla_o, gr, g_att_l, g_gla_l, w_out_l):
    B, L = att_o.shape[0], att_o.shape[1]
    att_o = rms_norm(att_o, g_att_l)
    gla_o = rms_norm(gla_o.astype(gr.dtype), g_gla_l).reshape(B, L, GLA_V_W) * jax.nn.silu(gr)
    return jnp.concatenate([att_o, gla_o], axis=-1) @ w_out_l


def conv_ffn(h, w_up_l, conv_w_l, conv_b_l, w_down_l):
    u = h @ w_up_l
    u = lax.conv_general_dilated(u, conv_w_l[:, None, :].astype(u.dtype), window_strides=(1,),
                                 padding=((CONV_WIDTH // 2, CONV_WIDTH // 2),),
                                 dimension_numbers=('NWC', 'WIO', 'NWC'),
                                 feature_group_count=u.shape[-1]) + conv_b_l
    val, gate = jnp.split(u, 2, axis=-1)
    return (jax.nn.silu(gate) * val) @ w_down_l


def setup_inputs(seed: int = 0) -> dict:
    key = jax.random.key(seed)
    ks = jax.random.split(key, 24)
    f32 = jnp.float32
    nrm = lambda k, shape, s: jax.random.normal(k, shape, f32) * s
    gain = lambda k, shape: 1.0 + 0.02 * jax.random.normal(k, shape, f32)
    L = DEPTH
    return {
        "x": nrm(ks[0], (BATCH, SEQ, D_MODEL), 1.0),
        "c": nrm(ks[1], (BATCH, D_MODEL), 1.0),
        "ctx": nrm(ks[2], (BATCH, CTX_LEN, D_MODEL), 1.0),
        "c_ctx": nrm(ks[3], (D_MODEL,), 1.0),
        "w_mod": nrm(ks[4], (L, D_MODEL, 6 * D_MODEL), 0.5 * D_MODEL ** -0.5),
        "b_mod": nrm(ks[5], (L, 6 * D_MODEL), 0.02),
        "g_norm1": gain(ks[6], (L, D_MODEL)),
        "w_in": nrm(ks[7], (L, D_MODEL, IN_W), D_MODEL ** -0.5),
        "g_q": gain(ks[8], (L, ATT_HEAD_DIM)),
        "g_k": gain(ks[9], (L, ATT_HEAD_DIM)),
        "w_gate_fwd": nrm(ks[10], (L, GLA_GATE_RANK, GLA_QK_W), GLA_GATE_RANK ** -0.5),
        "b_gate_fwd": nrm(ks[11], (L, GLA_QK_W), 0.1),
        "w_gate_bwd": nrm(ks[12], (L, GLA_GATE_RANK, GLA_QK_W), GLA_GATE_RANK ** -0.5),
        "b_gate_bwd": nrm(ks[13], (L, GLA_QK_W), 0.1),
        "g_att_out": gain(ks[14], (L, ATT_Q_W)),
        "g_gla_out": gain(ks[15], (L, GLA_VAL_DIM)),
        "w_out": nrm(ks[16], (L, MIX_W, D_MODEL), MIX_W ** -0.5),
        "g_norm2": gain(ks[17], (L, D_MODEL)),
        "w_up": nrm(ks[18], (L, D_MODEL, 2 * D_FF), D_MODEL ** -0.5),
        "conv_w": nrm(ks[19], (L, CONV_WIDTH, 2 * D_FF), CONV_WIDTH ** -0.5),
        "conv_b": nrm(ks[20], (L, 2 * D_FF), 0.02),
        "w_down": nrm(ks[21], (L, D_FF, D_MODEL), D_FF ** -0.5),
        "g_final": gain(ks[22], (D_MODEL,)),
    }


def reference(x, c, ctx, c_ctx, w_mod, b_mod, g_norm1, w_in, g_q, g_k, w_gate_fwd, b_gate_fwd,
              w_gate_bwd, b_gate_bwd, g_att_out, g_gla_out, w_out, g_norm2, w_up, conv_w, conv_b,
              w_down, g_final):
    B, S = x.shape[0], x.shape[1]
    rows = S // GRID_W
    cos, sin = axial_rope_tables(rows)
    zero_state = jnp.zeros((B, GLA_HEADS, GLA_KEY_DIM, GLA_VAL_DIM), jnp.float32)

    for i in range(DEPTH):
        sh1, sc1, gt1, sh2, sc2, gt2 = adaln(c, w_mod[i], b_mod[i])
        sh1, sc1, gt1, sh2, sc2, gt2 = (m[:, None, :] for m in (sh1, sc1, gt1, sh2, sc2, gt2))
        csh1, csc1, cgt1, csh2, csc2, cgt2 = adaln(c_ctx, w_mod[i], b_mod[i])
        layer_w = (w_in[i], g_q[i], g_k[i], w_gate_fwd[i], b_gate_fwd[i], w_gate_bwd[i], b_gate_bwd[i])

        hc = modulate(rms_norm(ctx, g_norm1[i]), csh1, csc1)
        cq, ck, cv, cgq, cgk, cgv, clf, clb, cgr = mixer_inputs(hc, *layer_w)
        co_gla, s_ctx_f, s_ctx_b = gla_bidirectional(cgq, cgk, cgv, clf, clb, zero_state, zero_state)

        h = modulate(rms_norm(x, g_norm1[i]), sh1, sc1)
        aq, ak, av, gq, gk, gv, lf, lb, gr = mixer_inputs(h, *layer_w)
        aq = apply_rope(aq, cos, sin)
        ak = apply_rope(ak, cos, sin)
        att_o = latent_attention(aq, ak, av, ck, cv)
        gla_o, _, _ = gla_bidirectional(gq, gk, gv, lf, lb, s_ctx_f, s_ctx_b)
        x = x + gt1 * merge_heads(att_o, gla_o, gr, g_att_out[i], g_gla_out[i], w_out[i])
        h2 = modulate(rms_norm(x, g_norm2[i]), sh2, sc2)
        x = x + gt2 * conv_ffn(h2, w_up[i], conv_w[i], conv_b[i], w_down[i])

        if i < DEPTH - 1:
            catt_o = context_attention(cq, ck, cv)
            ctx = ctx + cgt1 * merge_heads(catt_o, co_gla, cgr, g_att_out[i], g_gla_out[i], w_out[i])
            hc2 = modulate(rms_norm(ctx, g_norm2[i]), csh2, csc2)
            ctx = ctx + cgt2 * conv_ffn(hc2, w_up[i], conv_w[i], conv_b[i], w_down[i])

    return rms_norm(x, g_final)
```

```python
import numpy as np
from contextlib import ExitStack
import concourse.bass as bass
import concourse.mybir as mybir
from concourse.bass_utils import run_bass_kernel_spmd

F32 = mybir.dt.float32
BF16 = mybir.dt.bfloat16
ALU = mybir.AluOpType
AF = mybir.ActivationFunctionType
AX = mybir.AxisListType

D = 1024
NT = 16
NTA = 18
INW = 2336
DFF = 2816
NPAIR = 22
EPS = 1e-6
SLABS = [(0, 6), (6, 6), (12, 5), (17, 5)]
TG = [(0, 512), (510, 512), (1020, 512), (1530, 512), (2040, 10)]


class Buf:
    _n = 0

    def __init__(self, name=""):
        Buf._n += 1
        self.id = Buf._n
        self.name = name
        self.w = {}
        self.r = {}
        self.ps = None


class Prog:
    ENGS = ("pe", "act", "dve", "pool", "sp")
    NPS = 92

    def __init__(self, nc):
        self.nc = nc
        self.lists = {e: [] for e in self.ENGS}
        self.count = {e: 0 for e in self.ENGS}
        self.waited = {e: {} for e in self.ENGS}
        self.sems = {}
        self.free_ps = list(range(self.NPS))
        self.ps_cnt = [0] * self.NPS
        self.live = []

    def _deps(self, eng, reads, writes):
        deps = {}

        def add(k, v):
            if deps.get(k, 0) < v:
                deps[k] = v

        for b in reads:
            for k, v in b.w.items():
                add(k, v)
        strict = (eng == "pool")
        for b in writes:
            for k, v in b.w.items():
                if k != eng or strict:
                    add(k, v)
            for k, v in b.r.items():
                if k != eng or strict:
                    add(k, v)
        if eng == "pe":
            deps.pop("pe", None)
        out = []
        for k, v in deps.items():
            if self.waited[eng].get(k, 0) < v:
                self.waited[eng][k] = v
                out.append((k, v))
        return out

    def op(self, eng, fn, reads=(), writes=()):
        for k, v in self._deps(eng, reads, writes):
            self.lists[eng].append(("wait", k, v))
        self.count[eng] += 1
        c = self.count[eng]
        self.lists[eng].append(("op", fn))
        for b in reads:
            if b.r.get(eng, 0) < c:
                b.r[eng] = c
        for b in writes:
            b.w = {eng: c}
            b.r = {}

    def dma(self, eng, fn, reads=(), writes=(), sem_buf=None, inc=16):
        for k, v in self._deps(eng, reads, writes):
            self.lists[eng].append(("wait", k, v))
        sb = sem_buf if sem_buf is not None else (writes[0] if writes else reads[0])
        if sb.ps is None:
            cand = [p for p in self.free_ps if (p < 12) == (eng == "pool")]
            sb.ps = cand[0]
            self.free_ps.remove(sb.ps)
            self.live.append(sb)
        ps = sb.ps
        self.ps_cnt[ps] += inc
        key = ("dma", ps)
        c = self.ps_cnt[ps]
        self.lists[eng].append(("dma", fn, key, inc))
        for b in reads:
            if b.r.get(key, 0) < c:
                b.r[key] = c
        for b in writes:
            b.w = {key: c}
            b.r = {}

    def barrier(self):
        tot = dict(self.count)
        for ps in range(self.NPS):
            if self.ps_cnt[ps]:
                tot[("dma", ps)] = self.ps_cnt[ps]
        for e in self.ENGS:
            for k, v in tot.items():
                if k == e or v == 0:
                    continue
                if self.waited[e].get(k, 0) < v:
                    self.waited[e][k] = v
                    self.lists[e].append(("wait", k, v))
        for b in self.live:
            self.free_ps.append(b.ps)
            b.ps = None
        self.live = []

    def run(self, es):
        nc = self.nc
        for e in self.ENGS:
            self.sems[e] = es.enter_context(nc.semaphore("s_" + e))
        for ps in range(self.NPS):
            if self.ps_cnt[ps]:
                self.sems[("dma", ps)] = es.enter_context(nc.semaphore("d%d" % ps))
        block = es.enter_context(nc.Block())
        sems, lists = self.sems, self.lists
        targets = {e: set() for e in self.ENGS}
        for e in self.ENGS:
            for it in lists[e]:
                if it[0] == "wait" and it[1] in targets:
                    targets[it[1]].add(it[2])
        rank = {e: {v: i + 1 for i, v in enumerate(sorted(targets[e]))} for e in self.ENGS}

        def replay(name):
            def body(e):
                n = 0
                for it in lists[name]:
                    if it[0] == "wait":
                        k, v = it[1], it[2]
                        if k in rank:
                            v = rank[k][v]
                        e.wait_ge(sems[k], v)
                    elif it[0] == "op":
                        n += 1
                        ins = it[1](e)
                        if n in rank[name]:
                            ins.then_inc(sems[name], 1)
                    else:
                        it[1](e).then_inc(sems[it[2]], it[3])
            return body

        block.tensor(replay("pe"))
        block.scalar(replay("act"))
        block.vector(replay("dve"))
        block.gpsimd(replay("pool"))
        block.sync(replay("sp"))


class Rec:
    def __init__(self):
        self.items = []

    def op(self, *a, **k):
        self.items.append(("op", a, k))

    def dma(self, *a, **k):
        self.items.append(("dma", a, k))


def merge_emit_staged(P, main, side):
    stages = []
    prev = None
    for it in side.items:
        eng = it[1][0]
        if eng != prev:
            stages.append([])
            prev = eng
        stages[-1].append(it)
    nm = len(main.items)
    ns = len(stages)
    pos = [int((k + 0.5) * nm / ns) for k in range(ns)]
    k = 0
    for i, (kind, a, kw) in enumerate(main.items):
        while k < ns and pos[k] <= i:
            for (kd, a2, kw2) in stages[k]:
                getattr(P, kd)(*a2, **kw2)
            k += 1
        getattr(P, kind)(*a, **kw)
    while k < ns:
        for (kd, a2, kw2) in stages[k]:
            getattr(P, kd)(*a2, **kw2)
        k += 1


def merge_emit(P, recs, weights=None):
    idx = [0] * len(recs)
    n = [max(1, len(r.items)) * (weights[i] if weights else 1.0) for i, r in enumerate(recs)]
    while True:
        best = None
        for i, r in enumerate(recs):
            if idx[i] < len(r.items):
                frac = idx[i] / n[i]
                if best is None or frac < best[0]:
                    best = (frac, i)
        if best is None:
            break
        i = best[1]
        kind, a, k = recs[i].items[idx[i]]
        idx[i] += 1
        getattr(P, kind)(*a, **k)


def build(debug=None):
    nc = bass.Bass("TRN2", target_bir_lowering=False)
    P = Prog(nc)

    def din(name, shape, dt=F32):
        return nc.dram_tensor(name, list(shape), dt, kind="ExternalInput").ap()

    def dscr(name, shape, dt=F32):
        return nc.dram_tensor(name, list(shape), dt)

    x_in = din("x_own", [NT, 128, D])
    ctx_in = din("ctx_b", [2, 128, D])
    cT_in = din("cT", [128, 8, 2])
    wmod_in = din("w_mod_r", [12, 128, 8, 512])
    bmod_in = din("b_mod2", [2, 6144])
    g1_in = din("g1_bc", [128, D])
    g2_in = din("g2_bc", [128, D])
    gf_in = din("gf_bc", [128, D])
    win_in = din("w_in_r", [128, 8, INW])
    gqk_in = din("gqk_bc", [128, 640])
    cos_in = din("rope_cos", [128, NTA, 32])
    sin_in = din("rope_sin", [128, NTA, 32])
    wg_in = din("w_gate_aug", [33, 512])
    gatt_in = din("gatt_bc", [128, 512])
    ggla_in = din("ggla_bc", [128, 512])
    wout_in = din("w_out_r", [128, 8, D])
    wup_in = din("w_up_r", [NPAIR, 128, 8, 256])
    wdn_in = din("w_down_r", [NPAIR, 128, D])
    cw_in = din("conv_wT", [128, 2 * NPAIR, 3])
    cb_in = din("conv_bT", [128, 2 * NPAIR])
    flags_in = din("flags", [128, 8])
    sel8_in = din("sel8", [8, 2])
    identf_in = din("identf", [128, 128])
    tri_in = din("tri4", [128, 4, 128])
    mask8_in = din("mask8", [128, 8, 128])
    sel2_in = din("sel2", [2, 256])
    out_d = nc.dram_tensor("out", [NT, 128, D], F32, kind="ExternalOutput").ap()
    dbg_d = None
    if debug:
        dbg_d = nc.dram_tensor("dbg", list(debug[1]), F32, kind="ExternalOutput").ap()

    bc_scr = dscr("bc_scr", [5, 128, D])
    gla_scr = dscr("gla_scr", [NT, 128, 2560], BF16)
    x1_scr = dscr("x1_scr", [NT, 128, D])
    u_scr = dscr("u_scr", [NTA, 128, 512])
    B_uscr = [Buf() for _ in range(NTA)]
    cc1k_in = dscr("cc1k_in", [128, 1024])
    cc1k_out = dscr("cc1k_out", [512, 1024])
    cc1v_in = [dscr("cc1v_in%d" % i, [128, 520]) for i in range(2)]
    cc1v_out = [dscr("cc1v_out%d" % i, [512, 520]) for i in range(2)]
    B_cc1v_in = [Buf(), Buf()]
    B_cc1v_out = [Buf(), Buf()]
    cc2_in = dscr("cc2_in", [128, 516])
    cc2_out = dscr("cc2_out", [512, 516])
    cc3_in = dscr("cc3_in", [2, D], BF16)
    cc3_out = dscr("cc3_out", [8, D], BF16)
    B_bcscr = [Buf() for _ in range(5)]
    B_glascr = [Buf() for _ in range(NT)]
    B_glascr_a = [Buf() for _ in range(NT)]
    B_x1scr = [Buf() for _ in range(NT)]
    B_cc1in, B_cc1out, B_cc2in, B_cc2out, B_cc3in, B_cc3out = (Buf() for _ in range(6))
    B_out = Buf()
    B_outs = [Buf() for _ in range(NT)]
    GROUPS = [[0, 1, 2, 3], [4, 5, 6, 7]]

    es = ExitStack()
    with es:
        def sbuf(st, name, shape, dt=F32):
            return st.enter_context(nc.sbuf_tensor("sb_" + name, list(shape), dt)), Buf(name)

        psA = es.enter_context(nc.psum_tensor("psA", [128, 1024], F32))
        psB = es.enter_context(nc.psum_tensor("psB", [128, 1024], F32))
        psC = es.enter_context(nc.psum_tensor("psC", [128, 1024], F32))
        ps6 = es.enter_context(nc.psum_tensor("ps6", [128, 512], F32))
        ps7 = es.enter_context(nc.psum_tensor("ps7", [128, 512], F32))
        bank = [psA[:, 0:512], psA[:, 512:1024], psB[:, 0:512], psB[:, 512:1024],
                psC[:, 0:512], psC[:, 512:1024], ps6[:, :], ps7[:, :]]
        Bk = [Buf("bank%d" % i) for i in range(8)]
        tb6 = ps6[:, :].bitcast(BF16).rearrange("p (a b) -> p a b", b=128)

        identf, B_identf = sbuf(es, "identf", [128, 128])
        ident16, B_ident16 = sbuf(es, "ident16", [128, 128], BF16)
        tri, B_tri = sbuf(es, "tri", [128, 4, 128])
        mask8, B_mask8 = sbuf(es, "mask8", [128, 8, 128])
        onesf, B_onesf = sbuf(es, "onesf", [128, 2])
        sel2, B_sel2 = sbuf(es, "sel2", [2, 256])
        flags, B_flags = sbuf(es, "flags", [128, 8])
        epsc, B_epsc = sbuf(es, "epsc", [128, 1])
        P.dma("sp", lambda e: e.dma_start(out=identf[:], in_=identf_in[:, :]), writes=[B_identf])
        P.dma("sp", lambda e: e.dma_start(out=tri[:], in_=tri_in[:, :, :]), writes=[B_tri])
        P.dma("sp", lambda e: e.dma_start(out=mask8[:], in_=mask8_in[:, :, :]), writes=[B_mask8])
        P.dma("sp", lambda e: e.dma_start(out=sel2[:], in_=sel2_in[:, :]), writes=[B_sel2])
        P.dma("sp", lambda e: e.dma_start(out=flags[:], in_=flags_in[:, :]), writes=[B_flags])
        P.op("dve", lambda e: e.memset(onesf[:], 1.0), writes=[B_onesf])
        P.op("dve", lambda e: e.memset(epsc[:], EPS), writes=[B_epsc])
        P.op("dve", lambda e: e.tensor_copy(ident16[:], identf[:]), reads=[B_identf], writes=[B_ident16])

        h2T, B_h2T = sbuf(es, "h2T", [128, 8, 2050], BF16)
        scT, B_scT = sbuf(es, "scT", [128, 8, 2])
        stP = ExitStack()
        es.enter_context(stP)
        QT, B_QT = sbuf(stP, "QT", [128, NT, 4, 128], BF16)
        KTc, B_KTc = sbuf(stP, "KTc", [128, 256], BF16)
        Vc, B_Vc = sbuf(stP, "Vc", [128, 2, 130], BF16)
        sU = ExitStack()
        es.enter_context(sU)
        B_U = Buf("U")
        dec, B_dec = sbuf(sU, "dec", [128, NTA, 4])
        cst, _ = sbuf(sU, "cst", [128, 516])
        Sctx, _ = sbuf(sU, "Sctx", [128, 2, 256])
        Actx, _ = sbuf(sU, "Actx", [128, 2])
        B_cstd, B_Sctxd = [Buf(), Buf()], [Buf(), Buf()]

        st1 = ExitStack()
        es.enter_context(st1)
        gs1, B_gs1 = sbuf(st1, "gs1", [128, D])
        sh1, B_sh1 = sbuf(st1, "sh1", [128, D])
        cgs1, B_cgs1 = sbuf(st1, "cgs1", [128, D])
        csh1, B_csh1 = sbuf(st1, "csh1", [128, D])
        win16, B_win = sbuf(st1, "win16", [128, 8, INW], BF16)
        B_wink = [Buf() for _ in range(8)]

        def rstd_from_ss(ss_ap, rs_ap, n, Bss, Brs, tmp_ap, Btmp, T=None):
            T = T or P
            npart = ss_ap.shape[0]
            T.op("act", lambda e: e.activation(out=tmp_ap, in_=ss_ap, func=AF.Ln, bias=epsc[0:npart, :], scale=1.0 / n),
                 reads=[Bss, B_epsc], writes=[Btmp])
            T.op("act", lambda e: e.activation(out=rs_ap, in_=tmp_ap, func=AF.Exp, scale=-0.5), reads=[Btmp], writes=[Brs])

        def adaln_groups(T, ngs, wm, bm, mch, bct, gmul, B_gmul, psb):
            pm, pb0, pb1 = psb
            nb = 0
            for ng in ngs:
                w_t, B_w = wm[ng % 2]
                b_t, B_b = bm[ng % 2]
                m_t, B_m = mch[ng % 2]
                T.dma("sp", lambda e, w_t=w_t, ng=ng: e.dma_start(out=w_t[:], in_=wmod_in[ng, :, :, :]), writes=[B_w])
                T.dma("sp", lambda e, b_t=b_t, ng=ng: e.dma_start(out=b_t[:], in_=bmod_in[:, ng * 512:(ng + 1) * 512]), writes=[B_b])
                for k in range(8):
                    T.op("pe", lambda e, w_t=w_t, k=k: e.matmul(bank[pm][0:2, :], lhsT=scT[:, k, :], rhs=w_t[:, k, :],
                                                                start=(k == 0), stop=(k == 7)),
                         reads=[B_scT, B_w], writes=[Bk[pm]])
                T.op("dve", lambda e, m_t=m_t, b_t=b_t: e.tensor_tensor(out=m_t[:], in0=bank[pm][0:2, :], in1=b_t[:], op=ALU.add),
                     reads=[Bk[pm], B_b], writes=[B_m])
                sec, half = ng // 2, ng % 2
                hs = slice(half * 512, (half + 1) * 512)
                for row in range(2):
                    if row == 1 and sec > 1:
                        continue
                    pb = pb0 if row == 0 else pb1
                    T.op("pe", lambda e, m_t=m_t, row=row, pb=pb: e.matmul(bank[pb], lhsT=sel2[0:2, row * 128:(row + 1) * 128],
                                                                         rhs=m_t[:], start=True, stop=True),
                         reads=[B_sel2, B_m], writes=[Bk[pb]])
                    if row == 0 and sec in (0, 1):
                        dst, Bd = (sh1, B_sh1) if sec == 0 else (gs1, B_gs1)
                    elif row == 1:
                        dst, Bd = (csh1, B_csh1) if sec == 0 else (cgs1, B_cgs1)
                    else:
                        dst = None
                    if dst is not None:
                        if sec == 0:
                            T.op("act", lambda e, dst=dst, pb=pb, hs=hs: e.activation(out=dst[:, hs], in_=bank[pb], func=AF.Identity),
                                 reads=[Bk[pb]], writes=[Bd])
                        else:
                            T.op("dve", lambda e, dst=dst, pb=pb, hs=hs: e.scalar_tensor_tensor(
                                out=dst[:, hs], in0=bank[pb], scalar=1.0, in1=gmul[:, hs], op0=ALU.add, op1=ALU.mult),
                                reads=[Bk[pb], B_gmul], writes=[Bd])
                    else:
                        t_t, B_t = bct[nb % 2]
                        nb += 1
                        if sec == 4:
                            T.op("act", lambda e, t_t=t_t, pb=pb: e.activation(out=t_t[:], in_=bank[pb], func=AF.Identity, bias=onesf[:, 0:1], scale=1.0),
                                 reads=[Bk[pb], B_onesf], writes=[B_t])
                            T.op("pool", lambda e, t_t=t_t, hs=hs: e.tensor_tensor(out=t_t[:], in0=t_t[:], in1=gmul[:, hs], op=ALU.mult),
                                 reads=[B_t, B_gmul], writes=[B_t])
                        else:
                            T.op("act", lambda e, t_t=t_t, pb=pb: e.activation(out=t_t[:], in_=bank[pb], func=AF.Identity),
                                 reads=[Bk[pb]], writes=[B_t])
                        si = sec - 2
                        T.dma("sp", lambda e, t_t=t_t, si=si, hs=hs: e.dma_start(out=bc_scr[si, :, hs], in_=t_t[:]),
                              reads=[B_t], writes=[B_bcscr[si]])

        with ExitStack() as st0:
            cT, B_cT = sbuf(st0, "cT", [128, 8, 2])
            wm = [sbuf(st0, "wm%d" % i, [128, 8, 512]) for i in range(2)]
            bm = [sbuf(st0, "bm%d" % i, [2, 512]) for i in range(2)]
            mch = [sbuf(st0, "mch%d" % i, [2, 512]) for i in range(2)]
            gA, B_gA = sbuf(st0, "gA", [128, D])
            P.dma("sp", lambda e: e.dma_start(out=cT[:], in_=cT_in[:, :, :]), writes=[B_cT])
            P.dma("sp", lambda e: e.dma_start(out=gA[:], in_=g1_in[:, :]), writes=[B_gA])
            P.op("act", lambda e: e.activation(out=scT[:], in_=cT[:], func=AF.Silu), reads=[B_cT], writes=[B_scT])
            wstg = [sbuf(st0, "wstg%d" % i, [128, INW]) for i in range(2)]
            win_items = Rec()
            for k in range(8):
                ws_t, B_ws = wstg[k % 2]
                win_items.dma("sp", lambda e, k=k, ws_t=ws_t: e.dma_start(out=ws_t[:], in_=win_in[:, k, :]), writes=[B_ws])
                win_items.op("pool" if k % 2 == 0 else "dve", lambda e, k=k, ws_t=ws_t: e.tensor_copy(win16[:, k, :], ws_t[:]),
                             reads=[B_ws], writes=[B_wink[k]])
            T_ad0 = Rec()
            adaln_groups(T_ad0, range(0, 4), wm, bm, mch, None, gA, B_gA, (7, 5, 6))
            merge_emit(P, [T_ad0, win_items])
            P.barrier()

        if debug and debug[0] == "adaln":
            P.dma("sp", lambda e: e.dma_start(out=dbg_d[0], in_=gs1[:]), reads=[B_gs1], writes=[B_out])
            P.dma("sp", lambda e: e.dma_start(out=dbg_d[1], in_=sh1[:]), reads=[B_sh1], writes=[B_out])
            P.dma("sp", lambda e: e.dma_start(out=dbg_d[2], in_=cgs1[:]), reads=[B_cgs1], writes=[B_out])
            P.dma("sp", lambda e: e.dma_start(out=dbg_d[3], in_=csh1[:]), reads=[B_csh1], writes=[B_out])
            P.barrier()
            P.run(es)
            return nc

        def norm_mod_transpose(x_ap, Bx, gs_t, Bgs, sh_t, Bsh, hT_ap, BhT, wk, T=None, tbv=None, Btb=None, cp_eng="act"):
            T = T or P
            if tbv is None:
                tbv, Btb = tb6, Bk[6]
            junk, Bj, ss, Bss, tmp1, Bt1, rs, Brs, hf, Bhf, h16, Bh16 = wk
            T.op("act", lambda e: e.activation(out=junk[:], in_=x_ap, func=AF.Square, accum_out=ss[:, 0:1]),
                 reads=[Bx], writes=[Bj, Bss])
            rstd_from_ss(ss[:, 0:1], rs[:, 0:1], D, Bss, Brs, tmp1[:, 0:1], Bt1, T=T)
            T.op("dve", lambda e: e.scalar_tensor_tensor(out=hf[:], in0=x_ap, scalar=rs[:, 0:1], in1=gs_t[:],
                                                         op0=ALU.mult, op1=ALU.mult),
                 reads=[Bx, Brs, Bgs], writes=[Bhf])
            T.op("pool", lambda e: e.tensor_tensor(out=h16[:], in0=hf[:], in1=sh_t[:], op=ALU.add),
                 reads=[Bhf, Bsh], writes=[Bh16])
            for k in range(8):
                T.op("pe", lambda e, k=k: e.transpose(tbv[:, k, :], h16[:, k * 128:(k + 1) * 128], ident16[:]),
                     reads=[Bh16, B_ident16], writes=[Btb])
            if cp_eng == "act":
                T.op("act", lambda e: e.activation(out=hT_ap, in_=tbv, func=AF.Identity), reads=[Btb], writes=[BhT])
            else:
                T.op(cp_eng, lambda e: e.tensor_copy(hT_ap, tbv), reads=[Btb], writes=[BhT])

        with ExitStack() as sA:
            sW = ExitStack()
            sA.enter_context(sW)
            _sA = sA
            sA = sW
            wg16, B_wg = sbuf(sA, "wg16", [33, 512], BF16)
            gqk, B_gqk = sbuf(sA, "gqk", [128, 640])
            rcos, B_rcos = sbuf(sA, "rcos", [128, NTA, 32])
            rsin, B_rsin = sbuf(sA, "rsin", [128, NTA, 32])
            ggla, B_ggla = sbuf(sA, "ggla", [128, 512])
            xt = [sbuf(sA, "xt%d" % i, [128, D]) for i in range(2)]
            zr = [sbuf(sA, "z%d" % i, [128, INW])[0] for i in range(2)]
            B_zg = [[Buf() for _ in range(5)] for _ in range(2)]
            hfr = [sbuf(sA, "hf%d" % i, [128, D]) for i in range(2)]
            h16r = [sbuf(sA, "h16%d" % i, [128, D], BF16) for i in range(2)]
            hTr = [sbuf(sA, "hT%d" % i, [128, 8, 128], BF16) for i in range(2)]
            smF, B_smF = sbuf(sA, "smF", [128, 4])
            sm, B_sm = sbuf(sA, "sm", [128, 64])
            qg, B_qg = sbuf(sA, "qg", [128, 640])
            ro, B_ro = sbuf(sA, "ro", [128, 640])
            sq, B_sq = ro, B_ro
            rt = [sbuf(sA, "rt%d" % i, [128, 320]) for i in range(4)]
            qk16, B_qk16 = sbuf(sA, "qk16", [128, 640], BF16)
            kvst, B_kvst = sbuf(sA, "kvst", [128, 128 + 130], BF16)
            gl16r = [sbuf(sA, "gl16_%d" % i, [128, 33], BF16) for i in range(2)]
            glTr = [sbuf(sA, "glT_%d" % i, [33, 128], BF16) for i in range(2)]
            Lgr = [sbuf(sA, "Lg%d" % i, [128, 512]) for i in range(2)]
            E1F, B_E1F = sbuf(sA, "E1F", [128, 512])
            sgA, B_sgA = sbuf(sA, "sgA", [128, 512])
            sgr16, B_sgr16 = sbuf(sA, "sgr16", [128, 512], BF16)
            Eq, B_Eq = sbuf(sA, "Eq", [128, 512])
            Ek, B_Ek = sbuf(sA, "Ek", [128, 512])
            Er, B_Er = sbuf(sA, "Er", [128, 512])
            kup, B_kup = sbuf(sA, "kup", [128, 512], BF16)
            qkin, B_qkin = sbuf(sA, "qkin", [128, 4, 256], BF16)
            stg = [sbuf(sA, "stg0", [128, 2560], BF16)] * 2
            Ust = [sbuf(sA, "Ust%d" % i, [128, 2, 256]) for i in range(2)]
            tbF = psA[:, 0:512].bitcast(BF16).rearrange("p (a b) -> p a b", b=128)
            tb5 = psC[:, 512:1024].bitcast(BF16).rearrange("p (a b) -> p a b", b=128)
            MB = [0, 1, 7]

            P.dma("pool", lambda e: e.dma_start(out=wg16[:], in_=wg_in[:, :]), writes=[B_wg])
            P.dma("sp", lambda e: e.dma_start(out=gqk[:], in_=gqk_in[:, :]), writes=[B_gqk])
            P.dma("sp", lambda e: e.dma_start(out=rcos[:], in_=cos_in[:, :, :]), writes=[B_rcos])
            P.dma("sp", lambda e: e.dma_start(out=rsin[:], in_=sin_in[:, :, :]), writes=[B_rsin])
            P.dma("sp", lambda e: e.dma_start(out=ggla[:], in_=ggla_in[:, :]), writes=[B_ggla])
            for i_ in range(2):
                P.op("pool", lambda e, i_=i_: e.memset(gl16r[i_][0][:, 32:33], 1.0), writes=[gl16r[i_][1]])
            P.op("pool", lambda e: e.memset(kvst[:], 1.0), writes=[B_kvst])
            P.op("pool", lambda e: e.memset(Vc[:], 1.0), writes=[B_Vc])

            ZG = [(0, 512), (512, 512), (1024, 512), (1536, 512), (2048, 288)]

            def zB(tp, c0, c1):
                return [B_zg[tp][g] for g, (a0, an) in enumerate(ZG) if a0 < c1 and c0 < a0 + an]

            def build_FM(t):
                T = Rec()
                own = t >= 2
                j = t - 2
                tp = t % 2
                x_t, B_x = xt[tp]
                hf, B_hf = hfr[tp]
                h16, B_h16 = h16r[tp]
                hT, B_hT = hTr[tp]
                z = zr[tp]
                wk = (hf, B_hf, smF[:, 0:1], B_smF, smF[:, 1:2], B_smF, smF[:, 2:3], B_smF, hf, B_hf, h16, B_h16)
                src = x_in[j, :, :] if own else ctx_in[t, :, :]
                T.dma("sp", lambda e: e.dma_start(out=x_t[:], in_=src), writes=[B_x])
                if own:
                    norm_mod_transpose(x_t[:], B_x, gs1, B_gs1, sh1, B_sh1, hT[:], B_hT, wk, T=T, tbv=tbF, Btb=Bk[0])
                else:
                    norm_mod_transpose(x_t[:], B_x, cgs1, B_cgs1, csh1, B_csh1, hT[:], B_hT, wk, T=T, tbv=tbF, Btb=Bk[0])
                ng = 0
                for g, (c0, cn) in enumerate(ZG):
                    if not own and g in (0, 4):
                        continue
                    pb = MB[ng % 3]
                    ng += 1
                    for k in range(8):
                        T.op("pe", lambda e, pb=pb, c0=c0, cn=cn, k=k: e.matmul(bank[pb][:, 0:cn], lhsT=hT[:, k, :],
                                                                               rhs=win16[:, k, c0:c0 + cn],
                                                                               start=(k == 0), stop=(k == 7)),
                             reads=[B_hT, B_wink[k]], writes=[Bk[pb]])
                    T.op("act", lambda e, pb=pb, c0=c0, cn=cn: e.activation(out=z[:, c0:c0 + cn], in_=bank[pb][:, 0:cn], func=AF.Identity),
                         reads=[Bk[pb]], writes=[B_zg[tp][g]])
                gl16, B_gl16 = gl16r[tp]
                glT, B_glT = glTr[tp]
                Lg, B_Lg = Lgr[tp]
                T.op("pool", lambda e: e.tensor_copy(gl16[:, 0:32], z[:, 1792:1824]), reads=zB(tp, 1792, 1824), writes=[B_gl16])
                T.op("pe", lambda e: e.transpose(tbF[0:33, 0, :], gl16[:, :], ident16[:]), reads=[B_gl16, B_ident16], writes=[Bk[0]])
                T.op("act", lambda e: e.activation(out=glT[:], in_=tbF[0:33, 0, :], func=AF.Identity), reads=[Bk[0]], writes=[B_glT])
                T.op("pe", lambda e: e.matmul(bank[1], lhsT=glT[:], rhs=wg16[:], start=True, stop=True),
                     reads=[B_glT, B_wg], writes=[Bk[1]])
                T.op("act", lambda e: e.activation(out=E1F[:], in_=bank[1], func=AF.Exp, scale=-1.0), reads=[Bk[1]], writes=[B_E1F])
                T.op("act", lambda e: e.activation(out=Lg[:], in_=E1F[:], func=AF.Ln, bias=onesf[:, 0:1], scale=1.0),
                     reads=[B_E1F, B_onesf], writes=[B_Lg])
                return T

            def build_A(t):
                T = Rec()
                own = t >= 2
                j = t - 2
                tp = t % 2
                z = zr[tp]
                lo = 0 if own else 512
                nh = 10 if own else 2
                zs = z[:, lo:640]
                Bzs = zB(tp, lo, 640)
                T.op("pool", lambda e: e.tensor_tensor(out=sq[:, lo:640], in0=zs, in1=zs, op=ALU.mult),
                     reads=Bzs, writes=[B_sq])
                T.op("dve", lambda e: e.tensor_reduce(out=sm[:, 8:8 + nh],
                                                      in_=sq[:, lo:640].rearrange("p (h d) -> p h d", d=64),
                                                      axis=AX.X, op=ALU.add),
                     reads=[B_sq], writes=[B_sm])
                rstd_from_ss(sm[:, 8:8 + nh], sm[:, 32:32 + nh], 64, B_sm, B_sm, sm[:, 20:20 + nh], B_sm, T=T)
                T.op("pool", lambda e: e.tensor_tensor(out=qg[:, lo:640], in0=zs, in1=gqk[:, lo:640], op=ALU.mult),
                     reads=Bzs + [B_gqk], writes=[B_qg])
                qv = qg[:, lo:640].rearrange("p (h i two) -> p h i two", i=32, two=2)
                x0, x1 = qv[:, :, :, 0], qv[:, :, :, 1]
                cb = rcos[:, t, :].rearrange("p (o i) -> p o i", o=1).to_broadcast([128, nh, 32])
                sb_ = rsin[:, t, :].rearrange("p (o i) -> p o i", o=1).to_broadcast([128, nh, 32])
                r4 = [rt[i][0][:, 0:nh * 32].rearrange("p (h i) -> p h i", i=32) for i in range(4)]
                Br4 = [rt[i][1] for i in range(4)]
                T.op("dve", lambda e: e.tensor_tensor(out=r4[0], in0=x0, in1=cb, op=ALU.mult), reads=[B_qg, B_rcos], writes=[Br4[0]])
                T.op("pool", lambda e: e.tensor_tensor(out=r4[1], in0=x1, in1=sb_, op=ALU.mult), reads=[B_qg, B_rsin], writes=[Br4[1]])
                T.op("dve", lambda e: e.tensor_tensor(out=r4[2], in0=x0, in1=sb_, op=ALU.mult), reads=[B_qg, B_rsin], writes=[Br4[2]])
                T.op("pool", lambda e: e.tensor_tensor(out=r4[3], in0=x1, in1=cb, op=ALU.mult), reads=[B_qg, B_rcos], writes=[Br4[3]])
                rov = ro[:, lo:640].rearrange("p (h i two) -> p h i two", i=32, two=2)
                T.op("dve", lambda e: e.tensor_tensor(out=rov[:, :, :, 0], in0=r4[0], in1=r4[1], op=ALU.subtract),
                     reads=[Br4[0], Br4[1]], writes=[B_ro])
                T.op("pool", lambda e: e.tensor_tensor(out=rov[:, :, :, 1], in0=r4[2], in1=r4[3], op=ALU.add),
                     reads=[Br4[2], Br4[3]], writes=[B_ro])
                if own:
                    for g_ in range(2):
                        T.op("dve", lambda e, g_=g_: e.tensor_tensor(
                            out=qk16[:, 0:512].rearrange("p (i g d) -> p g i d", i=4, g=2)[:, g_, :, :],
                            in0=ro[:, g_ * 256:(g_ + 1) * 256].rearrange("p (i d) -> p i d", d=64),
                            in1=sm[:, 32 + 4 * g_:36 + 4 * g_].rearrange("p (i o) -> p i o", o=1).to_broadcast([128, 4, 64]),
                            op=ALU.mult), reads=[B_ro, B_sm], writes=[B_qk16])
                T.op("dve", lambda e: e.tensor_tensor(
                    out=qk16[:, 512:640].rearrange("p (h d) -> p h d", d=64),
                    in0=ro[:, 512:640].rearrange("p (h d) -> p h d", d=64),
                    in1=sm[:, 32 + nh - 2:32 + nh].rearrange("p (h o) -> p h o", o=1).to_broadcast([128, 2, 64]),
                    op=ALU.mult), reads=[B_ro, B_sm], writes=[B_qk16])
                blks = [0, 1, 2, 3, 4] if own else [4]
                for b in blks:
                    T.op("pe", lambda e, b=b: e.transpose(tb6[:, b, :], qk16[:, b * 128:(b + 1) * 128], ident16[:]),
                         reads=[B_qk16, B_ident16], writes=[Bk[6]])
                Bzv = zB(tp, 640, 768)
                if own:
                    T.op("act", lambda e: e.activation(out=QT[:, j, :, :], in_=tb6[:, 0:4, :], func=AF.Identity),
                         reads=[Bk[6]], writes=[B_QT])
                    T.op("act", lambda e: e.activation(out=kvst[:, 0:128], in_=tb6[:, 4, :], func=AF.Identity),
                         reads=[Bk[6]], writes=[B_kvst])
                    T.op("pool", lambda e: e.tensor_copy(kvst[:, 128:258].rearrange("p (g d) -> p g d", g=2)[:, :, 0:64],
                                                         z[:, 640:768].rearrange("p (g d) -> p g d", g=2)),
                         reads=Bzv, writes=[B_kvst])
                    T.dma("sp", lambda e: e.dma_start(out=cc1k_in[:, j * 64:(j + 1) * 64], in_=kvst[:, 0:128].bitcast(F32)),
                          reads=[B_kvst], writes=[B_cc1in])
                    T.dma("sp", lambda e: e.dma_start(out=cc1v_in[j // 8][:, (j % 8) * 65:(j % 8 + 1) * 65], in_=kvst[:, 128:258].bitcast(F32)),
                          reads=[B_kvst], writes=[B_cc1v_in[j // 8]])
                    Bgr = zB(tp, 1824, 2336)
                    T.op("act", lambda e: e.activation(out=sgA[:], in_=z[:, 1824:2336], func=AF.Exp, scale=-1.0), reads=Bgr, writes=[B_sgA])
                    T.op("dve", lambda e: e.tensor_scalar(out=sgA[:], in0=sgA[:], scalar1=1.0, scalar2=None, op0=ALU.add), reads=[B_sgA], writes=[B_sgA])
                    T.op("dve", lambda e: e.reciprocal(out=sgA[:], in_=sgA[:]), reads=[B_sgA], writes=[B_sgA])
                    T.op("pool", lambda e: e.tensor_tensor(out=sgA[:], in0=sgA[:], in1=z[:, 1824:2336], op=ALU.mult),
                         reads=[B_sgA] + Bgr, writes=[B_sgA])
                    T.op("pool", lambda e: e.tensor_tensor(out=sgr16[:], in0=sgA[:], in1=ggla[:], op=ALU.mult),
                         reads=[B_sgA, B_ggla], writes=[B_sgr16])
                    T.dma("sp", lambda e: e.dma_start(out=gla_scr[j, :, 2048:2560], in_=sgr16[:]), reads=[B_sgr16], writes=[B_glascr_a[j]])
                else:
                    T.op("act", lambda e: e.activation(out=KTc[:, t * 128:(t + 1) * 128], in_=tb6[:, 4, :], func=AF.Identity),
                         reads=[Bk[6]], writes=[B_KTc])
                    T.op("pool", lambda e: e.tensor_copy(Vc[:, t, :].rearrange("p (g d) -> p g d", g=2)[:, :, 0:64],
                                                         z[:, 640:768].rearrange("p (g d) -> p g d", g=2)),
                         reads=Bzv, writes=[B_Vc])
                return T

            def build_G(t):
                T = Rec()
                own = t >= 2
                j = t - 2
                tp = t % 2
                z = zr[tp]
                hT, B_hT = hTr[tp]
                Lg, B_Lg = Lgr[tp]
                T.op("pe", lambda e: e.matmul(bank[2][:, 0:256], lhsT=tri[:, 0, :], rhs=Lg[:, 0:256], start=True, stop=True),
                     reads=[B_tri, B_Lg], writes=[Bk[2]])
                T.op("pe", lambda e: e.matmul(bank[2][:, 256:512], lhsT=tri[:, 1, :], rhs=Lg[:, 256:512], start=True, stop=True),
                     reads=[B_tri, B_Lg], writes=[Bk[2]])
                T.op("pe", lambda e: e.matmul(bank[3][:, 0:256], lhsT=tri[:, 2, :], rhs=Lg[:, 0:256], start=True, stop=True),
                     reads=[B_tri, B_Lg], writes=[Bk[3]])
                T.op("pe", lambda e: e.matmul(bank[3][:, 256:512], lhsT=tri[:, 3, :], rhs=Lg[:, 256:512], start=True, stop=True),
                     reads=[B_tri, B_Lg], writes=[Bk[3]])
                for q4 in range(4):
                    T.op("pe", lambda e, q4=q4: e.matmul(bank[4][:, 2 * q4:2 * q4 + 2], lhsT=Lg[:, q4 * 128:(q4 + 1) * 128], rhs=onesf[:, 0:2],
                                                         start=True, stop=True),
                         reads=[B_Lg, B_onesf], writes=[Bk[4]])
                T.op("act", lambda e: e.activation(out=Er[:], in_=bank[3], func=AF.Exp, scale=-1.0 / 16), reads=[Bk[3]], writes=[B_Er])
                T.op("act", lambda e: e.activation(out=dec[:, t, :], in_=bank[4][:, 0:8].rearrange("p (q two) -> p q two", two=2)[:, :, 0],
                                                   func=AF.Exp, scale=-1.0 / 16),
                     reads=[Bk[4]], writes=[B_dec])
                if own:
                    T.op("act", lambda e: e.activation(out=Eq[:], in_=bank[2], func=AF.Exp, scale=-1.0 / 16), reads=[Bk[2]], writes=[B_Eq])
                    T.op("act", lambda e: e.activation(out=Ek[:], in_=bank[2], func=AF.Exp, scale=1.0 / 16), reads=[Bk[2]], writes=[B_Ek])
                gk = z[:, 1024:1280]
                Bgk = zB(tp, 1024, 1280)
                for d_ in range(2):
                    T.op("dve" if d_ == 0 else "pool",
                         lambda e, d_=d_: e.tensor_tensor(out=kup[:, d_ * 256:(d_ + 1) * 256], in0=gk,
                                                          in1=Er[:, d_ * 256:(d_ + 1) * 256], op=ALU.mult),
                         reads=Bgk + [B_Er], writes=[B_kup])
                s_t, B_s = stg[0]
                if own:
                    gq = z[:, 768:1024]
                    for d_ in range(2):
                        T.op("dve", lambda e, d_=d_: e.scalar_tensor_tensor(
                            out=qkin[:, d_, :], in0=gq, scalar=0.125, in1=Eq[:, d_ * 256:(d_ + 1) * 256], op0=ALU.mult, op1=ALU.mult),
                            reads=zB(tp, 768, 1024) + [B_Eq], writes=[B_qkin])
                        T.op("pool", lambda e, d_=d_: e.tensor_tensor(out=qkin[:, 2 + d_, :], in0=gk,
                                                                      in1=Ek[:, d_ * 256:(d_ + 1) * 256], op=ALU.mult),
                             reads=Bgk + [B_Ek], writes=[B_qkin])
                T.op("pool", lambda e: e.tensor_copy(s_t[:, 1536:2048], z[:, 1280:1792]), reads=zB(tp, 1280, 1792), writes=[B_s])
                v16 = s_t[:, 1536:2048]
                updv = psB[:, :].rearrange("p (d j c) -> p d j c", d=2, j=2)
                for d_ in range(2):
                    for jj in range(2):
                        T.op("pe", lambda e, d_=d_, jj=jj: e.matmul(
                            updv[:, d_, jj, :], lhsT=kup[:, d_ * 256 + jj * 128:d_ * 256 + (jj + 1) * 128],
                            rhs=v16[:, jj * 256:(jj + 1) * 256], start=True, stop=True),
                            reads=[B_kup, B_s], writes=[Bk[2 + d_]])
                U_t, B_Ut = Ust[t % 2]
                Uv = U_t[:].rearrange("p d (j c) -> p d j c", j=2)
                for d_ in range(2):
                    T.op("dve", lambda e, d_=d_: e.tensor_copy(Uv[0:64, d_, :, :], updv[0:64, d_, :, 0:128]),
                         reads=[Bk[2 + d_]], writes=[B_Ut])
                    T.op("dve", lambda e, d_=d_: e.tensor_copy(Uv[64:128, d_, :, :], updv[64:128, d_, :, 128:256]),
                         reads=[Bk[2 + d_]], writes=[B_Ut])
                T.dma("sp", lambda e: e.dma_start(out=u_scr[t, :, :], in_=U_t[:].rearrange("p d c -> p (d c)")),
                      reads=[B_Ut], writes=[B_uscr[t]])
                if own:
                    for a_ in range(4):
                        for jj in range(2):
                            T.op("pe", lambda e, a_=a_, jj=jj: e.transpose(tb5[:, a_ * 2 + jj, :], qkin[:, a_, jj * 128:(jj + 1) * 128], ident16[:]),
                                 reads=[B_qkin, B_ident16], writes=[Bk[5]])
                    T.op("act", lambda e: e.activation(out=s_t[:, 1024:1536].rearrange("p (a b) -> p a b", b=128),
                                                       in_=tb5[:, 0:4, :], func=AF.Identity),
                         reads=[Bk[5]], writes=[B_s])
                    T.op("act", lambda e: e.activation(out=hT[:, 0:4, :], in_=tb5[:, 4:8, :], func=AF.Identity),
                         reads=[Bk[5]], writes=[B_hT])
                    attv = psB[:, :].rearrange("p (a b) -> p a b", b=128)
                    for d_ in range(2):
                        for h in range(4):
                            jj, i = h // 2, h % 2
                            T.op("pe", lambda e, d_=d_, jj=jj, i=i: e.matmul(
                                attv[:, i * 4 + d_ * 2 + jj, :], lhsT=hT[i * 64:(i + 1) * 64, d_ * 2 + jj, :],
                                rhs=s_t[i * 64:(i + 1) * 64, 1024 + (d_ * 2 + jj) * 128:1024 + (d_ * 2 + jj + 1) * 128],
                                start=True, stop=True),
                                reads=[B_hT, B_s], writes=[Bk[2 + i]])
                    T.op("dve", lambda e: e.tensor_tensor(out=s_t[:, 0:1024].rearrange("p (a b) -> p a b", b=128),
                                                          in0=attv, in1=mask8[:], op=ALU.mult),
                         reads=[Bk[2], Bk[3], B_mask8], writes=[B_s])
                    T.dma("sp", lambda e: e.dma_start(out=gla_scr[j, :, 0:2048], in_=s_t[:, 0:2048]), reads=[B_s], writes=[B_glascr[j]])
                return T

            def build_C(t):
                T = Rec()
                U_t, B_Ut = Ust[t % 2]
                ctx_ = t < 2
                first = t in (0, 2)
                for d_ in range(2):
                    if ctx_:
                        cu, ca, Bc = Sctx[:, d_, :], Actx[:, :], B_Sctxd[d_]
                    else:
                        cu, ca, Bc = cst[:, 4 + d_ * 256:4 + (d_ + 1) * 256], cst[:, d_ * 2:d_ * 2 + 2], B_cstd[d_]
                    dcol = dec[:, t, d_ * 2:d_ * 2 + 2]
                    if first:
                        T.op("dve", lambda e, cu=cu, d_=d_: e.tensor_copy(cu, U_t[:, d_, :]), reads=[B_Ut], writes=[Bc])
                        if not (ctx_ and d_ == 0):
                            T.op("dve", lambda e, ca=ca, dcol=dcol: e.tensor_copy(ca, dcol), reads=[B_dec], writes=[Bc])
                        continue
                    for jj in range(2):
                        cs = cu[:, jj * 128:(jj + 1) * 128]
                        us = U_t[:, d_, jj * 128:(jj + 1) * 128]
                        if d_ == 0:
                            T.op("dve", lambda e, cs=cs, us=us, jj=jj, dcol=dcol: e.scalar_tensor_tensor(
                                out=cs, in0=cs, scalar=dcol[:, jj:jj + 1], in1=us, op0=ALU.mult, op1=ALU.add),
                                reads=[Bc, B_dec, B_Ut], writes=[Bc])
                        else:
                            T.op("dve", lambda e, cs=cs, us=us, jj=jj, ca=ca: e.scalar_tensor_tensor(
                                out=cs, in0=us, scalar=ca[:, jj:jj + 1], in1=cs, op0=ALU.mult, op1=ALU.add),
                                reads=[Bc, B_Ut], writes=[Bc])
                    if not (ctx_ and d_ == 0):
                        T.op("dve", lambda e, ca=ca, dcol=dcol: e.tensor_tensor(out=ca, in0=ca, in1=dcol, op=ALU.mult),
                             reads=[Bc, B_dec], writes=[Bc])
                return T

            merge_emit(P, [build_FM(0)])
            for t in range(NTA):
                chains, wts = [], []
                if t + 1 < NTA:
                    chains.append(build_FM(t + 1))
                    wts.append(0.5)
                chains.append(build_A(t))
                wts.append(1.0)
                chains.append(build_G(t))
                wts.append(1.5)
                if t >= 1:
                    chains.append(build_C(t - 1))
                    wts.append(0.8)
                merge_emit(P, chains, wts)
            merge_emit(P, [build_C(NTA - 1)])

            z, B_z = zr[(NTA - 1) % 2], B_zg[(NTA - 1) % 2][0]
            hf, B_hf = hfr[0]
            junk, B_junk = hfr[1]
            if debug and debug[0] == "stageA":
                P.dma("sp", lambda e: e.dma_start(out=dbg_d[0, :, 0:INW], in_=z[:]), reads=[B_z], writes=[B_out])
                P.dma("sp", lambda e: e.dma_start(out=dbg_d[1, :, 0:512], in_=Lgr[(NTA - 1) % 2][0][:]), reads=[Lgr[(NTA - 1) % 2][1]], writes=[B_out])
                P.dma("sp", lambda e: e.dma_start(out=dbg_d[2, :, 0:512], in_=Ust[(NTA - 1) % 2][0][:]), reads=[Ust[(NTA - 1) % 2][1]], writes=[B_out])
                P.dma("sp", lambda e: e.dma_start(out=dbg_d[3, :, 0:4], in_=dec[:, NTA - 1, :]), reads=[B_dec], writes=[B_out])
                P.op("dve", lambda e: e.tensor_copy(hf[:, 0:512], QT[:, NT - 1, :, :]), reads=[B_QT], writes=[B_hf])
                P.dma("sp", lambda e: e.dma_start(out=dbg_d[4, :, 0:512], in_=hf[:, 0:512]), reads=[B_hf], writes=[B_out])
                P.op("dve", lambda e: e.tensor_copy(junk[:, 0:258], kvst[:]), reads=[B_kvst], writes=[B_junk])
                P.dma("sp", lambda e: e.dma_start(out=dbg_d[5, :, 0:258], in_=junk[:, 0:258]), reads=[B_junk], writes=[B_out])
                P.barrier()
                P.run(es)
                return nc

            P.barrier()
            sW.close()
            st1.close()
            stQ = ExitStack()
            es.enter_context(stQ)
            Sst, _ = sbuf(stQ, "Sst", [128, NT, 2, 256], BF16)
            B_Sstd = [Buf(), Buf()]
            catg, B_catg = sbuf(stQ, "catg", [128, NT, 512], BF16)
            sA = _sA
            U, _ = sbuf(sA, "U", [128, NTA, 2, 256])
            P.dma("sp", lambda e: e.dma_start(out=U[:].rearrange("p t d c -> p t (d c)"), in_=u_scr.ap().rearrange("t p c -> p t c")),
                  reads=B_uscr, writes=[B_U])
            S2, _ = sbuf(sA, "S2", [128, 2, 256])
            G4, B_G4 = sbuf(sA, "G4", [128, 4, 516])
            tAd = [sbuf(sA, "tA%d" % i, [128, 4]) for i in range(2)]
            tUd = [sbuf(sA, "tU%d" % i, [128, 256]) for i in range(2)]
            B_S2d = [Buf(), Buf()]
            wm2 = [sbuf(sA, "wmb%d" % i, [128, 8, 512]) for i in range(2)]
            bm2 = [sbuf(sA, "bmb%d" % i, [2, 512]) for i in range(2)]
            mch2 = [sbuf(sA, "mchb%d" % i, [2, 512]) for i in range(2)]
            bct2 = [sbuf(sA, "bctb%d" % i, [128, 512]) for i in range(2)]
            gB, B_gB = sbuf(sA, "gB", [128, D])
            P.dma("sp", lambda e: e.dma_start(out=gB[:], in_=g2_in[:, :]), writes=[B_gB])
            ENGd = ["dve", "dve"]

            def step(T, eng, S_ap, BS, t, d_):
                for jj in range(2):
                    T.op(eng, lambda e, jj=jj: e.scalar_tensor_tensor(
                        out=S_ap[:, jj * 128:(jj + 1) * 128], in0=S_ap[:, jj * 128:(jj + 1) * 128],
                        scalar=dec[:, t, d_ * 2 + jj:d_ * 2 + jj + 1], in1=U[:, t, d_, jj * 128:(jj + 1) * 128],
                        op0=ALU.mult, op1=ALU.add), reads=[BS, B_dec, B_U], writes=[BS])

            def chain2(d_):
                T = Rec()
                eng = ENGd[d_]
                tA, B_tA = tAd[d_]
                tU, B_tU = tUd[d_]
                T.op(eng, lambda e: e.tensor_copy(S2[:, d_, :], Sctx[:, d_, :]), reads=[B_Sctxd[d_]], writes=[B_S2d[d_]])
                for i in (range(4) if d_ == 0 else range(3, -1, -1)):
                    fl = flags[:, d_ * 4 + i:d_ * 4 + i + 1]
                    T.op(eng, lambda e, i=i: e.tensor_scalar(out=tA[:, 0:2], in0=G4[:, i, d_ * 2:d_ * 2 + 2],
                                                             scalar1=-1.0, scalar2=None, op0=ALU.add),
                         reads=[B_G4], writes=[B_tA])
                    T.op(eng, lambda e, fl=fl: e.tensor_scalar(out=tA[:, 0:2], in0=tA[:, 0:2], scalar1=fl, scalar2=None, op0=ALU.mult),
                         reads=[B_tA, B_flags], writes=[B_tA])
                    T.op(eng, lambda e: e.tensor_scalar(out=tA[:, 2:4], in0=tA[:, 0:2], scalar1=1.0, scalar2=None, op0=ALU.add),
                         reads=[B_tA], writes=[B_tA])
                    T.op(eng, lambda e, i=i, fl=fl: e.tensor_scalar(out=tU[:], in0=G4[:, i, 4 + d_ * 256:4 + (d_ + 1) * 256],
                                                                    scalar1=fl, scalar2=None, op0=ALU.mult),
                         reads=[B_G4, B_flags], writes=[B_tU])
                    for jj in range(2):
                        T.op(eng, lambda e, jj=jj: e.scalar_tensor_tensor(
                            out=S2[:, d_, jj * 128:(jj + 1) * 128], in0=S2[:, d_, jj * 128:(jj + 1) * 128],
                            scalar=tA[:, 2 + jj:3 + jj], in1=tU[:, jj * 128:(jj + 1) * 128], op0=ALU.mult, op1=ALU.add),
                            reads=[B_S2d[d_], B_tA, B_tU], writes=[B_S2d[d_]])
                for t in (range(2, NTA) if d_ == 0 else range(NTA - 1, 1, -1)):
                    T.op("act", lambda e, t=t: e.activation(out=Sst[:, t - 2, d_, :], in_=S2[:, d_, :], func=AF.Identity),
                         reads=[B_S2d[d_]], writes=[B_Sstd[d_]])
                    step(T, eng, S2[:, d_, :], B_S2d[d_], t, d_)
                return T

            P.dma("sp", lambda e: e.dma_start(out=cc2_in[:, :], in_=cst[:]), reads=B_cstd, writes=[B_cc2in])
            P.dma("pool", lambda e: e.collective_compute("AllGather", ALU.bypass, replica_groups=GROUPS,
                                                         ins=[cc2_in.ap().opt()], outs=[cc2_out.ap().opt()]),
                  reads=[B_cc2in], writes=[B_cc2out], inc=1)
            P.dma("pool", lambda e: e.dma_start(out=G4[:], in_=cc2_out.ap().rearrange("(r p) c -> p r c", p=128)),
                  reads=[B_cc2out], writes=[B_G4])
            P.dma("pool", lambda e: e.collective_compute("AllGather", ALU.bypass, replica_groups=GROUPS,
                                                         ins=[cc1k_in.ap().opt()], outs=[cc1k_out.ap().opt()]),
                  reads=[B_cc1in], writes=[B_cc1out], inc=1)
            for hh in range(2):
                P.dma("pool", lambda e, hh=hh: e.collective_compute("AllGather", ALU.bypass, replica_groups=GROUPS,
                                                                    ins=[cc1v_in[hh].ap().opt()], outs=[cc1v_out[hh].ap().opt()]),
                      reads=[B_cc1v_in[hh]], writes=[B_cc1v_out[hh]], inc=1)
            T_ad1, T_ad2 = Rec(), Rec()
            adaln_groups(T_ad1, range(4, 8), wm2, bm2, mch2, bct2, gB, B_gB, (0, 1, 1))
            adaln_groups(T_ad2, range(8, 12), wm2, bm2, mch2, bct2, gB, B_gB, (0, 1, 1))
            merge_emit(P, [T_ad1])
            merge_emit(P, [chain2(0), chain2(1), T_ad2], [1.0, 1.0, 0.4])
            P.barrier()

        sTp = ExitStack()
        es.enter_context(sTp)
        KT, B_KT = sbuf(sTp, "KT", [128, 66 * 128], BF16)
        VA, B_VA = sbuf(sTp, "VA", [128, 66, 130], BF16)
        wout16, B_wout = sbuf(sTp, "wout16", [128, 8, D], BF16)
        gatt, B_gatt = sbuf(sTp, "gatt", [128, 512])
        gt1, B_gt1 = sbuf(sTp, "gt1", [128, D])
        gs2, B_gs2 = sbuf(sTp, "gs2", [128, D])
        sh2, B_sh2 = sbuf(sTp, "sh2", [128, D])
        xr = [sbuf(sTp, "xr%d" % i, [128, D]) for i in range(2)]
        P.dma("sp", lambda e: e.dma_start(out=gs2[:], in_=bc_scr[2, :, :]), reads=[B_bcscr[2]], writes=[B_gs2])
        P.dma("sp", lambda e: e.dma_start(out=sh2[:], in_=bc_scr[1, :, :]), reads=[B_bcscr[1]], writes=[B_sh2])
        P.dma("sp", lambda e: e.dma_start(out=gt1[:], in_=bc_scr[0, :, :]), reads=[B_bcscr[0]], writes=[B_gt1])
        for k in range(8):
            xs_t, B_xs = xr[k % 2]
            P.dma("sp", lambda e, k=k, xs_t=xs_t: e.dma_start(out=xs_t[:], in_=wout_in[:, k, :]), writes=[B_xs])
            P.op("dve", lambda e, k=k, xs_t=xs_t: e.tensor_tensor(out=wout16[:, k, :], in0=xs_t[:], in1=gt1[:], op=ALU.mult),
                 reads=[B_xs, B_gt1], writes=[B_wout])
        P.dma("sp", lambda e: e.dma_start(out=gatt[:], in_=gatt_in[:, :]), writes=[B_gatt])
        P.op("pool", lambda e: e.tensor_copy(KT[:, 0:256], KTc[:]), reads=[B_KTc], writes=[B_KT])
        P.op("pool", lambda e: e.tensor_copy(VA[:, 0:2, :], Vc[:]), reads=[B_Vc], writes=[B_VA])
        with ExitStack() as sG:
            NC2 = 4
            ld = [sbuf(sG, "ld%d" % i, [128, 2560], BF16) for i in range(NC2)]
            osqs = [sbuf(sG, "osq%d" % i, [128, 512]) for i in range(NC2)]
            ons = [sbuf(sG, "on%d" % i, [128, 512]) for i in range(NC2)]
            gss = [sbuf(sG, "gsm%d" % i, [128, 16]) for i in range(NC2)]
            B_catgt = [Buf() for _ in range(NT)]

            def build_P2(c):
                T = Rec()
                l_t, B_l = ld[c]
                osq, B_osq = osqs[c]
                on, B_on = ons[c]
                gs_, B_gs = gss[c]
                for j in range(c, NT, NC2):
                    T.dma("sp", lambda e, j=j: e.dma_start(out=l_t[:], in_=gla_scr[j, :, :]), reads=[B_glascr[j], B_glascr_a[j]], writes=[B_l])
                    for h in range(4):
                        jj, i = h // 2, h % 2
                        pb = 2 * c + i
                        oh = bank[pb][:, jj * 128:(jj + 1) * 128]
                        vh = l_t[:, 1536 + h * 128:1536 + (h + 1) * 128]
                        bf_ = i * 4 + jj
                        bb_ = i * 4 + 2 + jj
                        T.op("pe", lambda e, bf_=bf_, oh=oh, vh=vh: e.matmul(oh, lhsT=l_t[:, bf_ * 128:(bf_ + 1) * 128], rhs=vh, start=True, stop=False),
                             reads=[B_l], writes=[Bk[pb]])
                        T.op("pe", lambda e, bb_=bb_, oh=oh, vh=vh: e.matmul(oh, lhsT=l_t[:, bb_ * 128:(bb_ + 1) * 128], rhs=vh, start=False, stop=False),
                             reads=[B_l], writes=[Bk[pb]])
                        for d_ in range(2):
                            T.op("pe", lambda e, j=j, jj=jj, i=i, d_=d_, oh=oh: e.matmul(
                                oh, lhsT=l_t[i * 64:(i + 1) * 64, 1024 + (d_ * 2 + jj) * 128:1024 + (d_ * 2 + jj + 1) * 128],
                                rhs=Sst[i * 64:(i + 1) * 64, j, d_, jj * 128:(jj + 1) * 128], start=False, stop=(d_ == 1)),
                                reads=[B_l, B_Sstd[d_]], writes=[Bk[pb]])
                    for i in range(2):
                        pb = 2 * c + i
                        T.op("act", lambda e, i=i, pb=pb: e.activation(out=on[:].rearrange("p (jj i c) -> p i jj c", i=2, c=128)[:, i, :, :],
                                                                       in_=bank[pb][:, 0:256].rearrange("p (jj c) -> p jj c", c=128), func=AF.Identity),
                             reads=[Bk[pb]], writes=[B_on])
                    T.op("act", lambda e: e.activation(out=osq[:], in_=on[:], func=AF.Square), reads=[B_on], writes=[B_osq])
                    T.op("dve", lambda e: e.tensor_reduce(out=gs_[:, 0:4], in_=osq[:].rearrange("p (h d) -> p h d", d=128), axis=AX.X, op=ALU.add),
                         reads=[B_osq], writes=[B_gs])
                    rstd_from_ss(gs_[:, 0:4], gs_[:, 8:12], 128, B_gs, B_gs, gs_[:, 4:8], B_gs, T=T)
                    T.op("dve", lambda e: e.tensor_tensor(out=osq[:].rearrange("p (h d) -> p h d", d=128),
                                                          in0=on[:].rearrange("p (h d) -> p h d", d=128),
                                                          in1=gs_[:, 8:12].rearrange("p (h o) -> p h o", o=1).to_broadcast([128, 4, 128]),
                                                          op=ALU.mult), reads=[B_on, B_gs], writes=[B_osq])
                    T.op("dve" if (j // NC2) % 2 == 0 else "pool", lambda e, j=j: e.tensor_tensor(out=catg[:, j, :], in0=osq[:], in1=l_t[:, 2048:2560], op=ALU.mult),
                         reads=[B_osq, B_l], writes=[B_catgt[j]])
                return T

            merge_emit(P, [build_P2(c_) for c_ in range(NC2)])
            for r in range(4):
                P.dma("sp", lambda e, r=r: e.dma_start(out=KT[:, 256 + r * 2048:256 + (r + 1) * 2048].bitcast(F32),
                                                       in_=cc1k_out[r * 128:(r + 1) * 128, :]),
                      reads=[B_cc1out], writes=[B_KT])
                for hh in range(2):
                    t0_ = 2 + r * NT + hh * 8
                    P.dma("sp", lambda e, r=r, hh=hh, t0_=t0_: e.dma_start(
                        out=VA[:, t0_:t0_ + 8, :].rearrange("p t c -> p (t c)").bitcast(F32),
                        in_=cc1v_out[hh][r * 128:(r + 1) * 128, :]),
                        reads=[B_cc1v_out[hh]], writes=[B_VA])
            P.barrier()

        if debug and debug[0] == "gla":
            with ExitStack() as sD:
                dt_, B_dt = sbuf(sD, "dt_", [128, NT, 512])
                P.op("dve", lambda e: e.tensor_copy(dt_[:], catg[:]), reads=B_catgt, writes=[B_dt])
                P.dma("sp", lambda e: e.dma_start(out=dbg_d.rearrange("j p c -> p j c"), in_=dt_[:]), reads=[B_dt], writes=[B_out])
                P.barrier()
            P.run(es)
            return nc

        with ExitStack() as sT:
            PT = [sbuf(sT, "PT%d" % i, [128, 1024], BF16) for i in range(3)]
            oT, B_oT = sbuf(sT, "oT", [65, 2, 512])
            ao, B_ao = sbuf(sT, "ao", [128, 512])
            aj, B_aj = sbuf(sT, "aj", [128, 512])
            asm, B_asm = sbuf(sT, "asm", [128, 16])
            ca16, B_ca16 = sbuf(sT, "ca16", [128, 512], BF16)
            catT, B_catT = sbuf(sT, "catT", [128, 8, 128], BF16)
            x1t = [sbuf(sT, "x1t%d" % i, [128, D]) for i in range(2)]
            hfE, B_hfE = sbuf(sT, "hfE", [128, D])
            h16E, B_h16E = sbuf(sT, "h16E", [128, D], BF16)
            smE, B_smE = sbuf(sT, "smE", [128, 4])
            wkE = (hfE, B_hfE, smE[:, 0:1], B_smE, smE[:, 1:2], B_smE, smE[:, 2:3], B_smE, hfE, B_hfE, h16E, B_h16E)
            psS = [psA, psB]
            BS_ = [[Bk[0], Bk[1]], [Bk[2], Bk[3]]]
            def emit_qk(T, qb, kt):
                bi = kt % 2
                for g in range(2):
                    T.op("pe", lambda e, bi=bi, g=g, kt=kt, qb=qb: e.matmul(
                        psS[bi][:, g * 512:(g + 1) * 512], lhsT=KT[g * 64:(g + 1) * 64, kt * 128:(kt + 1) * 128],
                        rhs=QT[g * 64:(g + 1) * 64, qb, :, :], start=True, stop=True),
                        reads=[B_KT, B_QT], writes=[BS_[bi][g]])

            def build_main(qb):
                T = Rec()
                for kt in range(66):
                    bi = kt % 2
                    p_t, B_p = PT[kt % 3]
                    T.op("act", lambda e, bi=bi, p_t=p_t: e.activation(out=p_t[:], in_=psS[bi][:, :], func=AF.Exp, scale=0.125),
                         reads=BS_[bi], writes=[B_p])
                    if kt + 2 < 66:
                        emit_qk(T, qb, kt + 2)
                    elif qb + 1 < NT:
                        emit_qk(T, qb + 1, kt + 2 - 66)
                    for g in range(2):
                        T.op("pe", lambda e, g=g, kt=kt, p_t=p_t: e.matmul(
                            bank[4 + g][0:65, :], lhsT=VA[:, kt, g * 65:(g + 1) * 65], rhs=p_t[:, g * 512:(g + 1) * 512],
                            start=(kt == 0), stop=(kt == 65)),
                            reads=[B_VA, B_p], writes=[Bk[4 + g]])
                return T

            def build_E(qb):
                T = Rec()
                for g in range(2):
                    for i in range(4):
                        T.op("pe", lambda e, g=g, i=i: e.matmul(bank[6 + g][:, i * 128:i * 128 + 65], lhsT=oT[:, g, i * 128:(i + 1) * 128],
                                                               rhs=identf[0:65, 0:65], start=True, stop=True),
                             reads=[B_oT, B_identf], writes=[Bk[6 + g]])
                for g in range(2):
                    ov = bank[6 + g].rearrange("p (i c) -> p i c", c=128)
                    T.op("dve", lambda e, g=g, ov=ov: e.reciprocal(out=asm[:, g * 4:(g + 1) * 4].rearrange("p (i o) -> p i o", o=1), in_=ov[:, :, 64:65]),
                         reads=[Bk[6 + g]], writes=[B_asm])
                    T.op("dve", lambda e, g=g, ov=ov: e.tensor_tensor(
                        out=ao[:, g * 256:(g + 1) * 256].rearrange("p (i d) -> p i d", d=64), in0=ov[:, :, 0:64],
                        in1=asm[:, g * 4:(g + 1) * 4].rearrange("p (i o) -> p i o", o=1).to_broadcast([128, 4, 64]), op=ALU.mult),
                        reads=[Bk[6 + g], B_asm], writes=[B_ao])
                T.op("act", lambda e: e.activation(out=aj[:], in_=ao[:], func=AF.Square, accum_out=asm[:, 8:9]),
                     reads=[B_ao], writes=[B_aj, B_asm])
                rstd_from_ss(asm[:, 8:9], asm[:, 10:11], 512, B_asm, B_asm, asm[:, 9:10], B_asm, T=T)
                T.op("dve", lambda e: e.scalar_tensor_tensor(out=ca16[:], in0=ao[:], scalar=asm[:, 10:11], in1=gatt[:],
                                                             op0=ALU.mult, op1=ALU.mult),
                     reads=[B_ao, B_asm, B_gatt], writes=[B_ca16])
                for k in range(8):
                    src = ca16[:, k * 128:(k + 1) * 128] if k < 4 else catg[:, qb, (k - 4) * 128:(k - 3) * 128]
                    T.op("pe", lambda e, k=k, src=src: e.transpose(tb6[:, k, :], src, ident16[:]),
                         reads=[B_ca16, B_catgt[qb], B_ident16], writes=[Bk[6]])
                T.op("dve", lambda e: e.tensor_copy(catT[:], tb6), reads=[Bk[6]], writes=[B_catT])
                x_t, B_x = xr[qb % 2]
                x1_t, B_x1 = x1t[qb % 2]
                T.dma("sp", lambda e: e.dma_start(out=x_t[:], in_=x_in[qb, :, :]), writes=[B_x])
                for nh_ in range(2):
                    for k in range(8):
                        T.op("pe", lambda e, nh_=nh_, k=k: e.matmul(bank[7], lhsT=catT[:, k, :], rhs=wout16[:, k, nh_ * 512:(nh_ + 1) * 512],
                                                                    start=(k == 0), stop=(k == 7)),
                             reads=[B_catT, B_wout], writes=[Bk[7]])
                    hs = slice(nh_ * 512, (nh_ + 1) * 512)
                    T.op("dve", lambda e, hs=hs: e.tensor_tensor(out=x1_t[:, hs], in0=bank[7], in1=x_t[:, hs], op=ALU.add),
                         reads=[Bk[7], B_x], writes=[B_x1])
                T.dma("sp", lambda e: e.dma_start(out=x1_scr[qb, :, :], in_=x1_t[:]), reads=[B_x1], writes=[B_x1scr[qb]])
                norm_mod_transpose(x1_t[:], B_x1, gs2, B_gs2, sh2, B_sh2,
                                   h2T[:, :, 1 + qb * 128:1 + (qb + 1) * 128], B_h2T, wkE, T=T, cp_eng="dve")
                if qb == 0:
                    T.dma("sp", lambda e: e.dma_start(out=cc3_in[0:1, :], in_=h16E[0:1, :]), reads=[B_h16E], writes=[B_cc3in])
                if qb == NT - 1:
                    T.dma("sp", lambda e: e.dma_start(out=cc3_in[1:2, :], in_=h16E[127:128, :]), reads=[B_h16E], writes=[B_cc3in])
                return T

            emit_qk(P, 0, 0)
            emit_qk(P, 0, 1)
            merge_emit(P, [build_main(0)])
            for qb in range(NT):
                for g in range(2):
                    P.op("dve", lambda e, g=g: e.tensor_copy(oT[:, g, :], bank[4 + g][0:65, :]),
                         reads=[Bk[4 + g]], writes=[B_oT])
                if qb + 1 < NT:
                    merge_emit_staged(P, build_main(qb + 1), build_E(qb))
                else:
                    merge_emit(P, [build_E(qb)])
            P.barrier()
        sTp.close()
        stQ.close()
        sU.close()
        stP.close()

        if debug and debug[0] == "x1":
            with ExitStack() as sD:
                dt_, B_dt = sbuf(sD, "dt_", [128, NT, D])
                P.dma("sp", lambda e: e.dma_start(out=dt_[:], in_=x1_scr.ap().rearrange("j p c -> p j c")), reads=B_x1scr, writes=[B_dt])
                P.dma("sp", lambda e: e.dma_start(out=dbg_d.rearrange("j p c -> p j c"), in_=dt_[:]), reads=[B_dt], writes=[B_out])
                P.barrier()
            P.run(es)
            return nc

        with ExitStack() as sF:
            x1, B_x1r = sbuf(sF, "x1", [128, NT, D])
            B_x1tile = [Buf() for _ in range(NT)]
            gt2, B_gt2 = sbuf(sF, "gt2", [128, D])
            gfb, B_gfb = sbuf(sF, "gfb", [128, D])
            cw, B_cw = sbuf(sF, "cw", [128, 2 * NPAIR, 3])
            cbt, B_cbt = sbuf(sF, "cbt", [128, 2 * NPAIR])
            sel8, B_sel8 = sbuf(sF, "sel8", [8, 2])
            sel816, B_sel816 = sbuf(sF, "sel816", [8, 2], BF16)
            rows8, B_rows8 = sbuf(sF, "rows8", [8, D], BF16)
            hf, B_hf = sbuf(sF, "hfF", [128, D])
            junk, B_junk = hf, B_hf
            fsm, B_fsm = sbuf(sF, "fsm", [128, 8])
            P.dma("sp", lambda e: e.dma_start(out=gt2[:], in_=bc_scr[3, :, :]), reads=[B_bcscr[3]], writes=[B_gt2])
            P.dma("sp", lambda e: e.dma_start(out=cw[:], in_=cw_in[:, :, :]), writes=[B_cw])
            P.dma("sp", lambda e: e.dma_start(out=cbt[:], in_=cb_in[:, :]), writes=[B_cbt])
            P.dma("sp", lambda e: e.dma_start(out=sel8[:], in_=sel8_in[:, :]), writes=[B_sel8])
            P.op("dve", lambda e: e.tensor_copy(sel816[:], sel8[:]), reads=[B_sel8], writes=[B_sel816])
            for j in range(NT):
                P.dma("sp", lambda e, j=j: e.dma_start(out=x1[:, j, :], in_=x1_scr[j, :, :]), reads=[B_x1scr[j]], writes=[B_x1tile[j]])
            P.dma("sp", lambda e: e.dma_start(out=gfb[:], in_=gf_in[:, :]), writes=[B_gfb])
            P.dma("pool", lambda e: e.collective_compute("AllGather", ALU.bypass, replica_groups=GROUPS,
                                                         ins=[cc3_in.ap().opt()], outs=[cc3_out.ap().opt()]),
                  reads=[B_cc3in], writes=[B_cc3out], inc=1)
            P.dma("sp", lambda e: e.dma_start(out=rows8[:], in_=cc3_out[:, :]), reads=[B_cc3out], writes=[B_rows8])
            for k in range(8):
                P.op("pe", lambda e, k=k: e.matmul(bank[7][:, 2 * k:2 * k + 2], lhsT=rows8[:, k * 128:(k + 1) * 128], rhs=sel816[:],
                                                   start=True, stop=True), reads=[B_rows8, B_sel816], writes=[Bk[7]])
            hv = bank[7][:, 0:16].rearrange("p (k c) -> p k c", c=2)
            P.op("act", lambda e: e.activation(out=h2T[:, :, 0:1], in_=hv[:, :, 0:1], func=AF.Identity), reads=[Bk[7]], writes=[B_h2T])
            P.op("act", lambda e: e.activation(out=h2T[:, :, 2049:2050], in_=hv[:, :, 1:2], func=AF.Identity), reads=[Bk[7]], writes=[B_h2T])

            wu = [sbuf(sF, "wu%d" % i, [128, 8, 256], BF16) for i in range(2)]
            wst = [sbuf(sF, "wst%d" % i, [128, 2048]) for i in range(2)]
            nst = [0]
            wd = [sbuf(sF, "wd%d" % i, [128, 6, D], BF16) for i in range(2)]
            actT, B_actT = sbuf(sF, "actT", [128, 6, 2048], BF16)
            cva = [sbuf(sF, "cva%d" % i, [128, 512]) for i in range(2)]
            cga = [sbuf(sF, "cga%d" % i, [128, 512]) for i in range(2)]
            ot = [(hf, B_hf)] * 2
            psUV = [psA, psB]
            nit = 0

            def final_norm(j):
                o_t, B_o = ot[j % 2]
                P.op("act", lambda e: e.activation(out=junk[:], in_=x1[:, j, :], func=AF.Square, accum_out=fsm[:, 4:5]),
                     reads=[B_x1tile[j]], writes=[B_junk, B_fsm])
                rstd_from_ss(fsm[:, 4:5], fsm[:, 6:7], D, B_fsm, B_fsm, fsm[:, 5:6], B_fsm)
                P.op("dve", lambda e: e.scalar_tensor_tensor(out=o_t[:], in0=x1[:, j, :], scalar=fsm[:, 6:7], in1=gfb[:],
                                                             op0=ALU.mult, op1=ALU.mult),
                     reads=[B_x1tile[j], B_fsm, B_gfb], writes=[B_o])
                P.dma("sp", lambda e: e.dma_start(out=out_d[j, :, :], in_=o_t[:]), reads=[B_o], writes=[B_outs[j]])

            def load_wu(pj):
                wu_t, B_wu = wu[pj % 2]
                ws_t, B_ws = wst[nst[0] % 2]
                nst[0] += 1
                P.dma("sp", lambda e: e.dma_start(out=ws_t[:].rearrange("p (k c) -> p k c", c=256), in_=wup_in[pj, :, :, :]),
                      writes=[B_ws])
                P.op("act", lambda e: e.activation(out=wu_t[:], in_=ws_t[:].rearrange("p (k c) -> p k c", c=256), func=AF.Identity),
                     reads=[B_ws], writes=[B_wu])

            for si, (p0, pn) in enumerate(SLABS):
                wd_t, B_wd = wd[si % 2]
                for jj in range(0, pn, 2):
                    nj = min(2, pn - jj)
                    ws_t, B_ws = wst[nst[0] % 2]
                    nst[0] += 1
                    P.dma("sp", lambda e, ws_t=ws_t, jj=jj, p0=p0, nj=nj: e.dma_start(
                        out=ws_t[:, 0:nj * D].rearrange("p (a c) -> p a c", c=D),
                        in_=wdn_in[p0 + jj:p0 + jj + nj, :, :].rearrange("a p c -> p a c")), writes=[B_ws])
                    for a_ in range(nj):
                        P.op("pool", lambda e, ws_t=ws_t, wd_t=wd_t, jj=jj, a_=a_: e.tensor_tensor(
                            out=wd_t[:, jj + a_, :], in0=ws_t[:, a_ * D:(a_ + 1) * D], in1=gt2[:], op=ALU.mult),
                            reads=[B_ws, B_gt2], writes=[B_wd])
                for jj in range(pn):
                    pj = p0 + jj
                    wu_t, B_wu = wu[pj % 2]
                    if pj == 0:
                        load_wu(0)
                    if pj + 1 < NPAIR:
                        load_wu(pj + 1)
                    for (c0, n) in [TG[1], TG[2], TG[3], TG[0], TG[4]]:
                        bi = nit % 2
                        nit += 1
                        pv = psUV[bi][:, 0:n]
                        pg = psUV[bi][:, 512:512 + n]
                        Bpv, Bpg = Bk[2 * bi], Bk[2 * bi + 1]
                        for k in range(8):
                            P.op("pe", lambda e, pv=pv, wu_t=wu_t, k=k, c0=c0, n=n: e.matmul(pv, lhsT=wu_t[:, k, 0:128], rhs=h2T[:, k, c0:c0 + n],
                                                                                         start=(k == 0), stop=(k == 7)),
                                 reads=[B_wu, B_h2T], writes=[Bpv])
                        for k in range(8):
                            P.op("pe", lambda e, pg=pg, wu_t=wu_t, k=k, c0=c0, n=n: e.matmul(pg, lhsT=wu_t[:, k, 128:256], rhs=h2T[:, k, c0:c0 + n],
                                                                                         start=(k == 0), stop=(k == 7)),
                                 reads=[B_wu, B_h2T], writes=[Bpg])
                        m = n - 2
                        (va, Bva), (ga, Bga) = cva[bi], cga[bi]
                        for (pp, Bpp, ch, a_, Ba) in ((pv, Bpv, pj, va, Bva), (pg, Bpg, NPAIR + pj, ga, Bga)):
                            P.op("act", lambda e, pp=pp, ch=ch, a_=a_, m=m: e.activation(out=a_[:, 0:m], in_=pp[:, 1:1 + m], func=AF.Identity,
                                                                                       bias=cbt[:, ch:ch + 1], scale=cw[:, ch, 1:2]),
                                 reads=[Bpp, B_cbt, B_cw], writes=[Ba])
                            P.op("dve", lambda e, pp=pp, ch=ch, a_=a_, m=m: e.scalar_tensor_tensor(
                                out=a_[:, 0:m], in0=pp[:, 0:m], scalar=cw[:, ch, 0:1], in1=a_[:, 0:m], op0=ALU.mult, op1=ALU.add),
                                reads=[Bpp, B_cw, Ba], writes=[Ba])
                            P.op("dve", lambda e, pp=pp, ch=ch, a_=a_, m=m: e.scalar_tensor_tensor(
                                out=a_[:, 0:m], in0=pp[:, 2:2 + m], scalar=cw[:, ch, 2:3], in1=a_[:, 0:m], op0=ALU.mult, op1=ALU.add),
                                reads=[Bpp, B_cw, Ba], writes=[Ba])
                        P.op("act", lambda e, ga=ga, m=m: e.activation(out=ga[:, 0:m], in_=ga[:, 0:m], func=AF.Silu), reads=[Bga], writes=[Bga])
                        P.op("pool", lambda e, ga=ga, va=va, jj=jj, c0=c0, m=m: e.tensor_tensor(out=actT[:, jj, c0:c0 + m], in0=ga[:, 0:m], in1=va[:, 0:m], op=ALU.mult),
                             reads=[Bga, Bva], writes=[B_actT])
                for j in range(NT):
                    for nh_ in range(2):
                        hs = slice(nh_ * 512, (nh_ + 1) * 512)
                        pb = 4 + (2 * j + nh_) % 2
                        for jj in range(pn):
                            P.op("pe", lambda e, pb=pb, jj=jj, j=j, hs=hs, wd_t=wd_t, pn=pn: e.matmul(
                                bank[pb], lhsT=actT[:, jj, j * 128:(j + 1) * 128], rhs=wd_t[:, jj, hs], start=(jj == 0), stop=(jj == pn - 1)),
                                reads=[B_actT, B_wd], writes=[Bk[pb]])
                        P.op("dve", lambda e, pb=pb, j=j, hs=hs: e.tensor_tensor(out=x1[:, j, hs], in0=bank[pb], in1=x1[:, j, hs], op=ALU.add),
                             reads=[Bk[pb], B_x1tile[j]], writes=[B_x1tile[j]])
                    if si == len(SLABS) - 1:
                        final_norm(j)
            P.barrier()
        P.run(es)
    return nc


def _rope_tables(tok0):
    t = np.arange(tok0, tok0 + 2048, dtype=np.float32)
    row = np.floor(t / 64).astype(np.float32)
    col = (t - row * 64).astype(np.float32)
    freqs = (np.float32(10000.0) ** (-np.arange(0, 32, 2, dtype=np.float32) / np.float32(32))).astype(np.float32)
    ang = np.concatenate([row[:, None] * freqs, col[:, None] * freqs], axis=-1).astype(np.float32)
    return np.cos(ang).astype(np.float32), np.sin(ang).astype(np.float32)


def make_in_maps(x, c, ctx, c_ctx, w_mod, b_mod, g_norm1, w_in, g_q, g_k, w_gate_fwd, b_gate_fwd,
                 w_gate_bwd, b_gate_bwd, g_att_out, g_gla_out, w_out, g_norm2, w_up, conv_w, conv_b,
                 w_down, g_final):
    f = np.float32
    A = lambda a: np.ascontiguousarray(np.asarray(a, dtype=f))
    x, c, ctx, c_ctx = A(x), A(c), A(ctx), A(c_ctx)
    w_mod, b_mod, w_in, w_out, w_up, w_down = A(w_mod)[0], A(b_mod)[0], A(w_in)[0], A(w_out)[0], A(w_up)[0], A(w_down)[0]
    bc = lambda v, n=128: np.ascontiguousarray(np.broadcast_to(np.asarray(v, f)[None, :], (n, v.shape[-1])))
    shared = {}
    shared["w_mod_r"] = A(w_mod.reshape(8, 128, 12, 512).transpose(2, 1, 0, 3))
    shared["b_mod2"] = bc(b_mod, 2)
    shared["g1_bc"] = bc(A(g_norm1)[0])
    shared["g2_bc"] = bc(A(g_norm2)[0])
    shared["gf_bc"] = bc(A(g_final))
    shared["w_in_r"] = A(w_in.reshape(8, 128, INW).transpose(1, 0, 2))
    gq, gk = A(g_q)[0], A(g_k)[0]
    shared["gqk_bc"] = bc(np.concatenate([np.tile(gq, 8), np.tile(gk, 2)]))
    wg = np.zeros((33, 512), f)
    wg[0:16, 0:256] = A(w_gate_fwd)[0]
    wg[16:32, 256:512] = A(w_gate_bwd)[0]
    wg[32, 0:256] = A(b_gate_fwd)[0]
    wg[32, 256:512] = A(b_gate_bwd)[0]
    shared["w_gate_aug"] = wg
    shared["gatt_bc"] = bc(A(g_att_out)[0])
    shared["ggla_bc"] = bc(np.tile(A(g_gla_out)[0], 4))
    shared["w_out_r"] = A(w_out.reshape(8, 128, D).transpose(1, 0, 2))
    wu = w_up.reshape(8, 128, 2, NPAIR, 128)
    shared["w_up_r"] = A(wu.transpose(3, 1, 0, 2, 4).reshape(NPAIR, 128, 8, 256))
    shared["w_down_r"] = A(w_down.reshape(NPAIR, 128, D))
    cw = A(conv_w)[0]
    shared["conv_wT"] = A(cw.reshape(3, 2 * NPAIR, 128).transpose(2, 1, 0))
    shared["conv_bT"] = A(A(conv_b)[0].reshape(2 * NPAIR, 128).T)
    shared["identf"] = np.eye(128, dtype=f)
    s = np.arange(128)[:, None]
    t = np.arange(128)[None, :]
    tri = np.stack([(s <= t), (s >= t), (s > t), (s < t)], axis=1).astype(f)
    shared["tri4"] = A(tri)
    shared["mask8"] = A(np.stack([tri[:, 0], tri[:, 0], tri[:, 1], tri[:, 1], tri[:, 0], tri[:, 0], tri[:, 1], tri[:, 1]], axis=1))
    sel2 = np.zeros((2, 256), f)
    sel2[0, 0:128] = 1
    sel2[1, 128:256] = 1
    shared["sel2"] = sel2
    maps = []
    for core in range(8):
        b, r = core // 4, core % 4
        m = dict(shared)
        m["x_own"] = A(x[b, r * 2048:(r + 1) * 2048].reshape(NT, 128, D))
        m["ctx_b"] = A(ctx[b].reshape(2, 128, D))
        m["cT"] = A(np.stack([c[b], c_ctx], axis=0).reshape(2, 8, 128).transpose(2, 1, 0))
        cs, sn = _rope_tables(r * 2048)
        cs = np.concatenate([np.ones((256, 32), f), cs], axis=0).reshape(NTA, 128, 32).transpose(1, 0, 2)
        sn = np.concatenate([np.zeros((256, 32), f), sn], axis=0).reshape(NTA, 128, 32).transpose(1, 0, 2)
        m["rope_cos"] = A(cs)
        m["rope_sin"] = A(sn)
        fl = np.zeros((128, 8), f)
        for i in range(4):
            fl[:, i] = 1.0 if i < r else 0.0
            fl[:, 4 + i] = 1.0 if i > r else 0.0
        m["flags"] = fl
        s8 = np.zeros((8, 2), f)
        if r > 0:
            s8[(r - 1) * 2 + 1, 0] = 1.0
        if r < 3:
            s8[(r + 1) * 2 + 0, 1] = 1.0
        m["sel8"] = s8
        maps.append(m)
    return maps


_NC_CACHE = {}


def kernel(**inputs):
    if "nc" not in _NC_CACHE:
        _NC_CACHE["nc"] = build()
    nc = _NC_CACHE["nc"]
    maps = make_in_maps(**inputs)
    res = run_bass_kernel_spmd(nc, maps, core_ids=list(range(8)))
    out = np.zeros((2, 8192, D), np.float32)
    for core in range(8):
        b, r = core // 4, core % 4
        out[b, r * 2048:(r + 1) * 2048] = np.asarray(res.results[core]["out"]).reshape(2048, D)
    return out
```

```python
import numpy as np
from contextlib import ExitStack
import concourse.bass as bass
import concourse.mybir as mybir
from concourse.bass_utils import run_bass_kernel_spmd

F32 = mybir.dt.float32
BF16 = mybir.dt.bfloat16
ALU = mybir.AluOpType
AF = mybir.ActivationFunctionType
AX = mybir.AxisListType

D = 1024
NT = 16
NTA = 18
INW = 2336
DFF = 2816
NPAIR = 22
EPS = 1e-6
SLABS = [(0, 6), (6, 6), (12, 5), (17, 5)]
TG = [(0, 512), (510, 512), (1020, 512), (1530, 512), (2040, 10)]


class Buf:
    _n = 0

    def __init__(self, name=""):
        Buf._n += 1
        self.id = Buf._n
        self.name = name
        self.w = {}
        self.r = {}
        self.ps = None


class Prog:
    ENGS = ("pe", "act", "dve", "pool", "sp")
    NPS = 92

    def __init__(self, nc):
        self.nc = nc
        self.lists = {e: [] for e in self.ENGS}
        self.count = {e: 0 for e in self.ENGS}
        self.waited = {e: {} for e in self.ENGS}
        self.sems = {}
        self.free_ps = list(range(self.NPS))
        self.ps_cnt = [0] * self.NPS
        self.live = []

    def _deps(self, eng, reads, writes):
        deps = {}

        def add(k, v):
            if deps.get(k, 0) < v:
                deps[k] = v

        for b in reads:
            for k, v in b.w.items():
                add(k, v)
        strict = (eng == "pool")
        for b in writes:
            for k, v in b.w.items():
                if k != eng or strict:
                    add(k, v)
            for k, v in b.r.items():
                if k != eng or strict:
                    add(k, v)
        if eng == "pe":
            deps.pop("pe", None)
        out = []
        for k, v in deps.items():
            if self.waited[eng].get(k, 0) < v:
                self.waited[eng][k] = v
                out.append((k, v))
        return out

    def op(self, eng, fn, reads=(), writes=()):
        for k, v in self._deps(eng, reads, writes):
            self.lists[eng].append(("wait", k, v))
        self.count[eng] += 1
        c = self.count[eng]
        self.lists[eng].append(("op", fn))
        for b in reads:
            if b.r.get(eng, 0) < c:
                b.r[eng] = c
        for b in writes:
            b.w = {eng: c}
            b.r = {}

    def dma(self, eng, fn, reads=(), writes=(), sem_buf=None, inc=16):
        for k, v in self._deps(eng, reads, writes):
            self.lists[eng].append(("wait", k, v))
        sb = sem_buf if sem_buf is not None else (writes[0] if writes else reads[0])
        if sb.ps is None:
            cand = [p for p in self.free_ps if (p < 12) == (eng == "pool")]
            sb.ps = cand[0]
            self.free_ps.remove(sb.ps)
            self.live.append(sb)
        ps = sb.ps
        self.ps_cnt[ps] += inc
        key = ("dma", ps)
        c = self.ps_cnt[ps]
        self.lists[eng].append(("dma", fn, key, inc))
        for b in reads:
            if b.r.get(key, 0) < c:
                b.r[key] = c
        for b in writes:
            b.w = {key: c}
            b.r = {}

    def barrier(self):
        tot = dict(self.count)
        for ps in range(self.NPS):
            if self.ps_cnt[ps]:
                tot[("dma", ps)] = self.ps_cnt[ps]
        for e in self.ENGS:
            for k, v in tot.items():
                if k == e or v == 0:
                    continue
                if self.waited[e].get(k, 0) < v:
                    self.waited[e][k] = v
                    self.lists[e].append(("wait", k, v))
        for b in self.live:
            self.free_ps.append(b.ps)
            b.ps = None
        self.live = []

    def run(self, es):
        nc = self.nc
        for e in self.ENGS:
            self.sems[e] = es.enter_context(nc.semaphore("s_" + e))
        for ps in range(self.NPS):
            if self.ps_cnt[ps]:
                self.sems[("dma", ps)] = es.enter_context(nc.semaphore("d%d" % ps))
        block = es.enter_context(nc.Block())
        sems, lists = self.sems, self.lists
        targets = {e: set() for e in self.ENGS}
        for e in self.ENGS:
            for it in lists[e]:
                if it[0] == "wait" and it[1] in targets:
                    targets[it[1]].add(it[2])
        rank = {e: {v: i + 1 for i, v in enumerate(sorted(targets[e]))} for e in self.ENGS}

        def replay(name):
            def body(e):
                n = 0
                for it in lists[name]:
                    if it[0] == "wait":
                        k, v = it[1], it[2]
                        if k in rank:
                            v = rank[k][v]
                        e.wait_ge(sems[k], v)
                    elif it[0] == "op":
                        n += 1
                        ins = it[1](e)
                        if n in rank[name]:
                            ins.then_inc(sems[name], 1)
                    else:
                        it[1](e).then_inc(sems[it[2]], it[3])
            return body

        block.tensor(replay("pe"))
        block.scalar(replay("act"))
        block.vector(replay("dve"))
        block.gpsimd(replay("pool"))
        block.sync(replay("sp"))


class Rec:
    def __init__(self):
        self.items = []

    def op(self, *a, **k):
        self.items.append(("op", a, k))

    def dma(self, *a, **k):
        self.items.append(("dma", a, k))


def merge_emit_staged(P, main, side):
    stages = []
    prev = None
    for it in side.items:
        eng = it[1][0]
        if eng != prev:
            stages.append([])
            prev = eng
        stages[-1].append(it)
    nm = len(main.items)
    ns = len(stages)
    pos = [int((k + 0.5) * nm / ns) for k in range(ns)]
    k = 0
    for i, (kind, a, kw) in enumerate(main.items):
        while k < ns and pos[k] <= i:
            for (kd, a2, kw2) in stages[k]:
                getattr(P, kd)(*a2, **kw2)
            k += 1
        getattr(P, kind)(*a, **kw)
    while k < ns:
        for (kd, a2, kw2) in stages[k]:
            getattr(P, kd)(*a2, **kw2)
        k += 1


def merge_emit(P, recs, weights=None):
    idx = [0] * len(recs)
    n = [max(1, len(r.items)) * (weights[i] if weights else 1.0) for i, r in enumerate(recs)]
    while True:
        best = None
        for i, r in enumerate(recs):
            if idx[i] < len(r.items):
                frac = idx[i] / n[i]
                if best is None or frac < best[0]:
                    best = (frac, i)
        if best is None:
            break
        i = best[1]
        kind, a, k = recs[i].items[idx[i]]
        idx[i] += 1
        getattr(P, kind)(*a, **k)


def build(debug=None):
    nc = bass.Bass("TRN2", target_bir_lowering=False)
    P = Prog(nc)

    def din(name, shape, dt=F32):
        return nc.dram_tensor(name, list(shape), dt, kind="ExternalInput").ap()

    def dscr(name, shape, dt=F32):
        return nc.dram_tensor(name, list(shape), dt)

    x_in = din("x_own", [NT, 128, D])
    ctx_in = din("ctx_b", [2, 128, D])
    cT_in = din("cT", [128, 8, 2])
    wmod_in = din("w_mod_r", [12, 128, 8, 512])
    bmod_in = din("b_mod2", [2, 6144])
    g1_in = din("g1_bc", [128, D])
    g2_in = din("g2_bc", [128, D])
    gf_in = din("gf_bc", [128, D])
    win_in = din("w_in_r", [128, 8, INW])
    gqk_in = din("gqk_bc", [128, 640])
    cos_in = din("rope_cos", [128, NTA, 32])
    sin_in = din("rope_sin", [128, NTA, 32])
    wg_in = din("w_gate_aug", [33, 512])
    gatt_in = din("gatt_bc", [128, 512])
    ggla_in = din("ggla_bc", [128, 512])
    wout_in = din("w_out_r", [128, 8, D])
    wup_in = din("w_up_r", [NPAIR, 128, 8, 256])
    wdn_in = din("w_down_r", [NPAIR, 128, D])
    cw_in = din("conv_wT", [128, 2 * NPAIR, 3])
    cb_in = din("conv_bT", [128, 2 * NPAIR])
    flags_in = din("flags", [128, 8])
    sel8_in = din("sel8", [8, 2])
    identf_in = din("identf", [128, 128])
    tri_in = din("tri4", [128, 4, 128])
    mask8_in = din("mask8", [128, 8, 128])
    sel2_in = din("sel2", [2, 256])
    out_d = nc.dram_tensor("out", [NT, 128, D], F32, kind="ExternalOutput").ap()
    dbg_d = None
    if debug:
        dbg_d = nc.dram_tensor("dbg", list(debug[1]), F32, kind="ExternalOutput").ap()

    bc_scr = dscr("bc_scr", [5, 128, D])
    gla_scr = dscr("gla_scr", [NT, 128, 2560], BF16)
    x1_scr = dscr("x1_scr", [NT, 128, D])
    u_scr = dscr("u_scr", [NTA, 128, 512])
    B_uscr = [Buf() for _ in range(NTA)]
    cc1k_in = dscr("cc1k_in", [128, 1024])
    cc1k_out = dscr("cc1k_out", [512, 1024])
    cc1v_in = [dscr("cc1v_in%d" % i, [128, 520]) for i in range(2)]
    cc1v_out = [dscr("cc1v_out%d" % i, [512, 520]) for i in range(2)]
    B_cc1v_in = [Buf(), Buf()]
    B_cc1v_out = [Buf(), Buf()]
    cc2_in = dscr("cc2_in", [128, 516])
    cc2_out = dscr("cc2_out", [512, 516])
    cc3_in = dscr("cc3_in", [2, D], BF16)
    cc3_out = dscr("cc3_out", [8, D], BF16)
    B_bcscr = [Buf() for _ in range(5)]
    B_glascr = [Buf() for _ in range(NT)]
    B_glascr_a = [Buf() for _ in range(NT)]
    B_x1scr = [Buf() for _ in range(NT)]
    B_cc1in, B_cc1out, B_cc2in, B_cc2out, B_cc3in, B_cc3out = (Buf() for _ in range(6))
    B_out = Buf()
    B_outs = [Buf() for _ in range(NT)]
    GROUPS = [[0, 1, 2, 3], [4, 5, 6, 7]]

    es = ExitStack()
    with es:
        def sbuf(st, name, shape, dt=F32):
            return st.enter_context(nc.sbuf_tensor("sb_" + name, list(shape), dt)), Buf(name)

        psA = es.enter_context(nc.psum_tensor("psA", [128, 1024], F32))
        psB = es.enter_context(nc.psum_tensor("psB", [128, 1024], F32))
        psC = es.enter_context(nc.psum_tensor("psC", [128, 1024], F32))
        ps6 = es.enter_context(nc.psum_tensor("ps6", [128, 512], F32))
        ps7 = es.enter_context(nc.psum_tensor("ps7", [128, 512], F32))
        bank = [psA[:, 0:512], psA[:, 512:1024], psB[:, 0:512], psB[:, 512:1024],
                psC[:, 0:512], psC[:, 512:1024], ps6[:, :], ps7[:, :]]
        Bk = [Buf("bank%d" % i) for i in range(8)]
        tb6 = ps6[:, :].bitcast(BF16).rearrange("p (a b) -> p a b", b=128)

        identf, B_identf = sbuf(es, "identf", [128, 128])
        ident16, B_ident16 = sbuf(es, "ident16", [128, 128], BF16)
        tri, B_tri = sbuf(es, "tri", [128, 4, 128])
        mask8, B_mask8 = sbuf(es, "mask8", [128, 8, 128])
        onesf, B_onesf = sbuf(es, "onesf", [128, 2])
        sel2, B_sel2 = sbuf(es, "sel2", [2, 256])
        flags, B_flags = sbuf(es, "flags", [128, 8])
        epsc, B_epsc = sbuf(es, "epsc", [128, 1])
        P.dma("sp", lambda e: e.dma_start(out=identf[:], in_=identf_in[:, :]), writes=[B_identf])
        P.dma("sp", lambda e: e.dma_start(out=tri[:], in_=tri_in[:, :, :]), writes=[B_tri])
        P.dma("sp", lambda e: e.dma_start(out=mask8[:], in_=mask8_in[:, :, :]), writes=[B_mask8])
        P.dma("sp", lambda e: e.dma_start(out=sel2[:], in_=sel2_in[:, :]), writes=[B_sel2])
        P.dma("sp", lambda e: e.dma_start(out=flags[:], in_=flags_in[:, :]), writes=[B_flags])
        P.op("dve", lambda e: e.memset(onesf[:], 1.0), writes=[B_onesf])
        P.op("dve", lambda e: e.memset(epsc[:], EPS), writes=[B_epsc])
        P.op("dve", lambda e: e.tensor_copy(ident16[:], identf[:]), reads=[B_identf], writes=[B_ident16])

        h2T, B_h2T = sbuf(es, "h2T", [128, 8, 2050], BF16)
        scT, B_scT = sbuf(es, "scT", [128, 8, 2])
        stP = ExitStack()
        es.enter_context(stP)
        QT, B_QT = sbuf(stP, "QT", [128, NT, 4, 128], BF16)
        KTc, B_KTc = sbuf(stP, "KTc", [128, 256], BF16)
        Vc, B_Vc = sbuf(stP, "Vc", [128, 2, 130], BF16)
        sU = ExitStack()
        es.enter_context(sU)
        B_U = Buf("U")
        dec, B_dec = sbuf(sU, "dec", [128, NTA, 4])
        cst, _ = sbuf(sU, "cst", [128, 516])
        Sctx, _ = sbuf(sU, "Sctx", [128, 2, 256])
        Actx, _ = sbuf(sU, "Actx", [128, 2])
        B_cstd, B_Sctxd = [Buf(), Buf()], [Buf(), Buf()]

        st1 = ExitStack()
        es.enter_context(st1)
        gs1, B_gs1 = sbuf(st1, "gs1", [128, D])
        sh1, B_sh1 = sbuf(st1, "sh1", [128, D])
        cgs1, B_cgs1 = sbuf(st1, "cgs1", [128, D])
        csh1, B_csh1 = sbuf(st1, "csh1", [128, D])
        win16, B_win = sbuf(st1, "win16", [128, 8, INW], BF16)
        B_wink = [Buf() for _ in range(8)]

        def rstd_from_ss(ss_ap, rs_ap, n, Bss, Brs, tmp_ap, Btmp, T=None):
            T = T or P
            npart = ss_ap.shape[0]
            T.op("act", lambda e: e.activation(out=tmp_ap, in_=ss_ap, func=AF.Ln, bias=epsc[0:npart, :], scale=1.0 / n),
                 reads=[Bss, B_epsc], writes=[Btmp])
            T.op("act", lambda e: e.activation(out=rs_ap, in_=tmp_ap, func=AF.Exp, scale=-0.5), reads=[Btmp], writes=[Brs])

        def adaln_groups(T, ngs, wm, bm, mch, bct, gmul, B_gmul, psb):
            pm, pb0, pb1 = psb
            nb = 0
            for ng in ngs:
                w_t, B_w = wm[ng % 2]
                b_t, B_b = bm[ng % 2]
                m_t, B_m = mch[ng % 2]
                T.dma("sp", lambda e, w_t=w_t, ng=ng: e.dma_start(out=w_t[:], in_=wmod_in[ng, :, :, :]), writes=[B_w])
                T.dma("sp", lambda e, b_t=b_t, ng=ng: e.dma_start(out=b_t[:], in_=bmod_in[:, ng * 512:(ng + 1) * 512]), writes=[B_b])
                for k in range(8):
                    T.op("pe", lambda e, w_t=w_t, k=k: e.matmul(bank[pm][0:2, :], lhsT=scT[:, k, :], rhs=w_t[:, k, :],
                                                                start=(k == 0), stop=(k == 7)),
                         reads=[B_scT, B_w], writes=[Bk[pm]])
                T.op("dve", lambda e, m_t=m_t, b_t=b_t: e.tensor_tensor(out=m_t[:], in0=bank[pm][0:2, :], in1=b_t[:], op=ALU.add),
                     reads=[Bk[pm], B_b], writes=[B_m])
                sec, half = ng // 2, ng % 2
                hs = slice(half * 512, (half + 1) * 512)
                for row in range(2):
                    if row == 1 and sec > 1:
                        continue
                    pb = pb0 if row == 0 else pb1
                    T.op("pe", lambda e, m_t=m_t, row=row, pb=pb: e.matmul(bank[pb], lhsT=sel2[0:2, row * 128:(row + 1) * 128],
                                                                         rhs=m_t[:], start=True, stop=True),
                         reads=[B_sel2, B_m], writes=[Bk[pb]])
                    if row == 0 and sec in (0, 1):
                        dst, Bd = (sh1, B_sh1) if sec == 0 else (gs1, B_gs1)
                    elif row == 1:
                        dst, Bd = (csh1, B_csh1) if sec == 0 else (cgs1, B_cgs1)
                    else:
                        dst = None
                    if dst is not None:
                        if sec == 0:
                            T.op("act", lambda e, dst=dst, pb=pb, hs=hs: e.activation(out=dst[:, hs], in_=bank[pb], func=AF.Identity),
                                 reads=[Bk[pb]], writes=[Bd])
                        else:
                            T.op("dve", lambda e, dst=dst, pb=pb, hs=hs: e.scalar_tensor_tensor(
                                out=dst[:, hs], in0=bank[pb], scalar=1.0, in1=gmul[:, hs], op0=ALU.add, op1=ALU.mult),
                                reads=[Bk[pb], B_gmul], writes=[Bd])
                    else:
                        t_t, B_t = bct[nb % 2]
                        nb += 1
                        if sec == 4:
                            T.op("act", lambda e, t_t=t_t, pb=pb: e.activation(out=t_t[:], in_=bank[pb], func=AF.Identity, bias=onesf[:, 0:1], scale=1.0),
                                 reads=[Bk[pb], B_onesf], writes=[B_t])
                            T.op("pool", lambda e, t_t=t_t, hs=hs: e.tensor_tensor(out=t_t[:], in0=t_t[:], in1=gmul[:, hs], op=ALU.mult),
                                 reads=[B_t, B_gmul], writes=[B_t])
                        else:
                            T.op("act", lambda e, t_t=t_t, pb=pb: e.activation(out=t_t[:], in_=bank[pb], func=AF.Identity),
                                 reads=[Bk[pb]], writes=[B_t])
                        si = sec - 2
                        T.dma("sp", lambda e, t_t=t_t, si=si, hs=hs: e.dma_start(out=bc_scr[si, :, hs], in_=t_t[:]),
                              reads=[B_t], writes=[B_bcscr[si]])

        with ExitStack() as st0:
            cT, B_cT = sbuf(st0, "cT", [128, 8, 2])
            wm = [sbuf(st0, "wm%d" % i, [128, 8, 512]) for i in range(2)]
            bm = [sbuf(st0, "bm%d" % i, [2, 512]) for i in range(2)]
            mch = [sbuf(st0, "mch%d" % i, [2, 512]) for i in range(2)]
            gA, B_gA = sbuf(st0, "gA", [128, D])
            P.dma("sp", lambda e: e.dma_start(out=cT[:], in_=cT_in[:, :, :]), writes=[B_cT])
            P.dma("sp", lambda e: e.dma_start(out=gA[:], in_=g1_in[:, :]), writes=[B_gA])
            P.op("act", lambda e: e.activation(out=scT[:], in_=cT[:], func=AF.Silu), reads=[B_cT], writes=[B_scT])
            wstg = [sbuf(st0, "wstg%d" % i, [128, INW]) for i in range(2)]
            win_items = Rec()
            for k in range(8):
                ws_t, B_ws = wstg[k % 2]
                win_items.dma("sp", lambda e, k=k, ws_t=ws_t: e.dma_start(out=ws_t[:], in_=win_in[:, k, :]), writes=[B_ws])
                win_items.op("pool" if k % 2 == 0 else "dve", lambda e, k=k, ws_t=ws_t: e.tensor_copy(win16[:, k, :], ws_t[:]),
                             reads=[B_ws], writes=[B_wink[k]])
            T_ad0 = Rec()
            adaln_groups(T_ad0, range(0, 4), wm, bm, mch, None, gA, B_gA, (7, 5, 6))
            merge_emit(P, [T_ad0, win_items])
            P.barrier()

        if debug and debug[0] == "adaln":
            P.dma("sp", lambda e: e.dma_start(out=dbg_d[0], in_=gs1[:]), reads=[B_gs1], writes=[B_out])
            P.dma("sp", lambda e: e.dma_start(out=dbg_d[1], in_=sh1[:]), reads=[B_sh1], writes=[B_out])
            P.dma("sp", lambda e: e.dma_start(out=dbg_d[2], in_=cgs1[:]), reads=[B_cgs1], writes=[B_out])
            P.dma("sp", lambda e: e.dma_start(out=dbg_d[3], in_=csh1[:]), reads=[B_csh1], writes=[B_out])
            P.barrier()
            P.run(es)
            return nc

        def norm_mod_transpose(x_ap, Bx, gs_t, Bgs, sh_t, Bsh, hT_ap, BhT, wk, T=None, tbv=None, Btb=None, cp_eng="act"):
            T = T or P
            if tbv is None:
                tbv, Btb = tb6, Bk[6]
            junk, Bj, ss, Bss, tmp1, Bt1, rs, Brs, hf, Bhf, h16, Bh16 = wk
            T.op("act", lambda e: e.activation(out=junk[:], in_=x_ap, func=AF.Square, accum_out=ss[:, 0:1]),
                 reads=[Bx], writes=[Bj, Bss])
            rstd_from_ss(ss[:, 0:1], rs[:, 0:1], D, Bss, Brs, tmp1[:, 0:1], Bt1, T=T)
            T.op("dve", lambda e: e.scalar_tensor_tensor(out=hf[:], in0=x_ap, scalar=rs[:, 0:1], in1=gs_t[:],
                                                         op0=ALU.mult, op1=ALU.mult),
                 reads=[Bx, Brs, Bgs], writes=[Bhf])
            T.op("pool", lambda e: e.tensor_tensor(out=h16[:], in0=hf[:], in1=sh_t[:], op=ALU.add),
                 reads=[Bhf, Bsh], writes=[Bh16])
            for k in range(8):
                T.op("pe", lambda e, k=k: e.transpose(tbv[:, k, :], h16[:, k * 128:(k + 1) * 128], ident16[:]),
                     reads=[Bh16, B_ident16], writes=[Btb])
            if cp_eng == "act":
                T.op("act", lambda e: e.activation(out=hT_ap, in_=tbv, func=AF.Identity), reads=[Btb], writes=[BhT])
            else:
                T.op(cp_eng, lambda e: e.tensor_copy(hT_ap, tbv), reads=[Btb], writes=[BhT])

        with ExitStack() as sA:
            sW = ExitStack()
            sA.enter_context(sW)
            _sA = sA
            sA = sW
            wg16, B_wg = sbuf(sA, "wg16", [33, 512], BF16)
            gqk, B_gqk = sbuf(sA, "gqk", [128, 640])
            rcos, B_rcos = sbuf(sA, "rcos", [128, NTA, 32])
            rsin, B_rsin = sbuf(sA, "rsin", [128, NTA, 32])
            ggla, B_ggla = sbuf(sA, "ggla", [128, 512])
            xt = [sbuf(sA, "xt%d" % i, [128, D]) for i in range(2)]
            zr = [sbuf(sA, "z%d" % i, [128, INW])[0] for i in range(2)]
            B_zg = [[Buf() for _ in range(5)] for _ in range(2)]
            hfr = [sbuf(sA, "hf%d" % i, [128, D]) for i in range(2)]
            h16r = [sbuf(sA, "h16%d" % i, [128, D], BF16) for i in range(2)]
            hTr = [sbuf(sA, "hT%d" % i, [128, 8, 128], BF16) for i in range(2)]
            smF, B_smF = sbuf(sA, "smF", [128, 4])
            sm, B_sm = sbuf(sA, "sm", [128, 64])
            qg, B_qg = sbuf(sA, "qg", [128, 640])
            ro, B_ro = sbuf(sA, "ro", [128, 640])
            sq, B_sq = ro, B_ro
            rt = [sbuf(sA, "rt%d" % i, [128, 320]) for i in range(4)]
            qk16, B_qk16 = sbuf(sA, "qk16", [128, 640], BF16)
            kvst, B_kvst = sbuf(sA, "kvst", [128, 128 + 130], BF16)
            gl16r = [sbuf(sA, "gl16_%d" % i, [128, 33], BF16) for i in range(2)]
            glTr = [sbuf(sA, "glT_%d" % i, [33, 128], BF16) for i in range(2)]
            Lgr = [sbuf(sA, "Lg%d" % i, [128, 512]) for i in range(2)]
            E1F, B_E1F = sbuf(sA, "E1F", [128, 512])
            sgA, B_sgA = sbuf(sA, "sgA", [128, 512])
            sgr16, B_sgr16 = sbuf(sA, "sgr16", [128, 512], BF16)
            Eq, B_Eq = sbuf(sA, "Eq", [128, 512])
            Ek, B_Ek = sbuf(sA, "Ek", [128, 512])
            Er, B_Er = sbuf(sA, "Er", [128, 512])
            kup, B_kup = sbuf(sA, "kup", [128, 512], BF16)
            qkin, B_qkin = sbuf(sA, "qkin", [128, 4, 256], BF16)
            stg = [sbuf(sA, "stg0", [128, 2560], BF16)] * 2
            Ust = [sbuf(sA, "Ust%d" % i, [128, 2, 256]) for i in range(2)]
            tbF = psA[:, 0:512].bitcast(BF16).rearrange("p (a b) -> p a b", b=128)
            tb5 = psC[:, 512:1024].bitcast(BF16).rearrange("p (a b) -> p a b", b=128)
            MB = [0, 1, 7]

            P.dma("pool", lambda e: e.dma_start(out=wg16[:], in_=wg_in[:, :]), writes=[B_wg])
            P.dma("sp", lambda e: e.dma_start(out=gqk[:], in_=gqk_in[:, :]), writes=[B_gqk])
            P.dma("sp", lambda e: e.dma_start(out=rcos[:], in_=cos_in[:, :, :]), writes=[B_rcos])
            P.dma("sp", lambda e: e.dma_start(out=rsin[:], in_=sin_in[:, :, :]), writes=[B_rsin])
            P.dma("sp", lambda e: e.dma_start(out=ggla[:], in_=ggla_in[:, :]), writes=[B_ggla])
            for i_ in range(2):
                P.op("pool", lambda e, i_=i_: e.memset(gl16r[i_][0][:, 32:33], 1.0), writes=[gl16r[i_][1]])
            P.op("pool", lambda e: e.memset(kvst[:], 1.0), writes=[B_kvst])
            P.op("pool", lambda e: e.memset(Vc[:], 1.0), writes=[B_Vc])

            ZG = [(0, 512), (512, 512), (1024, 512), (1536, 512), (2048, 288)]

            def zB(tp, c0, c1):
                return [B_zg[tp][g] for g, (a0, an) in enumerate(ZG) if a0 < c1 and c0 < a0 + an]

            def build_FM(t):
                T = Rec()
                own = t >= 2
                j = t - 2
                tp = t % 2
                x_t, B_x = xt[tp]
                hf, B_hf = hfr[tp]
                h16, B_h16 = h16r[tp]
                hT, B_hT = hTr[tp]
                z = zr[tp]
                wk = (hf, B_hf, smF[:, 0:1], B_smF, smF[:, 1:2], B_smF, smF[:, 2:3], B_smF, hf, B_hf, h16, B_h16)
                src = x_in[j, :, :] if own else ctx_in[t, :, :]
                T.dma("sp", lambda e: e.dma_start(out=x_t[:], in_=src), writes=[B_x])
                if own:
                    norm_mod_transpose(x_t[:], B_x, gs1, B_gs1, sh1, B_sh1, hT[:], B_hT, wk, T=T, tbv=tbF, Btb=Bk[0])
                else:
                    norm_mod_transpose(x_t[:], B_x, cgs1, B_cgs1, csh1, B_csh1, hT[:], B_hT, wk, T=T, tbv=tbF, Btb=Bk[0])
                ng = 0
                for g, (c0, cn) in enumerate(ZG):
                    if not own and g in (0, 4):
                        continue
                    pb = MB[ng % 3]
                    ng += 1
                    for k in range(8):
                        T.op("pe", lambda e, pb=pb, c0=c0, cn=cn, k=k: e.matmul(bank[pb][:, 0:cn], lhsT=hT[:, k, :],
                                                                               rhs=win16[:, k, c0:c0 + cn],
                                                                               start=(k == 0), stop=(k == 7)),
                             reads=[B_hT, B_wink[k]], writes=[Bk[pb]])
                    T.op("act", lambda e, pb=pb, c0=c0, cn=cn: e.activation(out=z[:, c0:c0 + cn], in_=bank[pb][:, 0:cn], func=AF.Identity),
                         reads=[Bk[pb]], writes=[B_zg[tp][g]])
                gl16, B_gl16 = gl16r[tp]
                glT, B_glT = glTr[tp]
                Lg, B_Lg = Lgr[tp]
                T.op("pool", lambda e: e.tensor_copy(gl16[:, 0:32], z[:, 1792:1824]), reads=zB(tp, 1792, 1824), writes=[B_gl16])
                T.op("pe", lambda e: e.transpose(tbF[0:33, 0, :], gl16[:, :], ident16[:]), reads=[B_gl16, B_ident16], writes=[Bk[0]])
                T.op("act", lambda e: e.activation(out=glT[:], in_=tbF[0:33, 0, :], func=AF.Identity), reads=[Bk[0]], writes=[B_glT])
                T.op("pe", lambda e: e.matmul(bank[1], lhsT=glT[:], rhs=wg16[:], start=True, stop=True),
                     reads=[B_glT, B_wg], writes=[Bk[1]])
                T.op("act", lambda e: e.activation(out=E1F[:], in_=bank[1], func=AF.Exp, scale=-1.0), reads=[Bk[1]], writes=[B_E1F])
                T.op("act", lambda e: e.activation(out=Lg[:], in_=E1F[:], func=AF.Ln, bias=onesf[:, 0:1], scale=1.0),
                     reads=[B_E1F, B_onesf], writes=[B_Lg])
                return T

            def build_A(t):
                T = Rec()
                own = t >= 2
                j = t - 2
                tp = t % 2
                z = zr[tp]
                lo = 0 if own else 512
                nh = 10 if own else 2
                zs = z[:, lo:640]
                Bzs = zB(tp, lo, 640)
                T.op("pool", lambda e: e.tensor_tensor(out=sq[:, lo:640], in0=zs, in1=zs, op=ALU.mult),
                     reads=Bzs, writes=[B_sq])
                T.op("dve", lambda e: e.tensor_reduce(out=sm[:, 8:8 + nh],
                                                      in_=sq[:, lo:640].rearrange("p (h d) -> p h d", d=64),
                                                      axis=AX.X, op=ALU.add),
                     reads=[B_sq], writes=[B_sm])
                rstd_from_ss(sm[:, 8:8 + nh], sm[:, 32:32 + nh], 64, B_sm, B_sm, sm[:, 20:20 + nh], B_sm, T=T)
                T.op("pool", lambda e: e.tensor_tensor(out=qg[:, lo:640], in0=zs, in1=gqk[:, lo:640], op=ALU.mult),
                     reads=Bzs + [B_gqk], writes=[B_qg])
                qv = qg[:, lo:640].rearrange("p (h i two) -> p h i two", i=32, two=2)
                x0, x1 = qv[:, :, :, 0], qv[:, :, :, 1]
                cb = rcos[:, t, :].rearrange("p (o i) -> p o i", o=1).to_broadcast([128, nh, 32])
                sb_ = rsin[:, t, :].rearrange("p (o i) -> p o i", o=1).to_broadcast([128, nh, 32])
                r4 = [rt[i][0][:, 0:nh * 32].rearrange("p (h i) -> p h i", i=32) for i in range(4)]
                Br4 = [rt[i][1] for i in range(4)]
                T.op("dve", lambda e: e.tensor_tensor(out=r4[0], in0=x0, in1=cb, op=ALU.mult), reads=[B_qg, B_rcos], writes=[Br4[0]])
                T.op("pool", lambda e: e.tensor_tensor(out=r4[1], in0=x1, in1=sb_, op=ALU.mult), reads=[B_qg, B_rsin], writes=[Br4[1]])
                T.op("dve", lambda e: e.tensor_tensor(out=r4[2], in0=x0, in1=sb_, op=ALU.mult), reads=[B_qg, B_rsin], writes=[Br4[2]])
                T.op("pool", lambda e: e.tensor_tensor(out=r4[3], in0=x1, in1=cb, op=ALU.mult), reads=[B_qg, B_rcos], writes=[Br4[3]])
                rov = ro[:, lo:640].rearrange("p (h i two) -> p h i two", i=32, two=2)
                T.op("dve", lambda e: e.tensor_tensor(out=rov[:, :, :, 0], in0=r4[0], in1=r4[1], op=ALU.subtract),
                     reads=[Br4[0], Br4[1]], writes=[B_ro])
                T.op("pool", lambda e: e.tensor_tensor(out=rov[:, :, :, 1], in0=r4[2], in1=r4[3], op=ALU.add),
                     reads=[Br4[2], Br4[3]], writes=[B_ro])
                if own:
                    for g_ in range(2):
                        T.op("dve", lambda e, g_=g_: e.tensor_tensor(
                            out=qk16[:, 0:512].rearrange("p (i g d) -> p g i d", i=4, g=2)[:, g_, :, :],
                            in0=ro[:, g_ * 256:(g_ + 1) * 256].rearrange("p (i d) -> p i d", d=64),
                            in1=sm[:, 32 + 4 * g_:36 + 4 * g_].rearrange("p (i o) -> p i o", o=1).to_broadcast([128, 4, 64]),
                            op=ALU.mult), reads=[B_ro, B_sm], writes=[B_qk16])
                T.op("dve", lambda e: e.tensor_tensor(
                    out=qk16[:, 512:640].rearrange("p (h d) -> p h d", d=64),
                    in0=ro[:, 512:640].rearrange("p (h d) -> p h d", d=64),
                    in1=sm[:, 32 + nh - 2:32 + nh].rearrange("p (h o) -> p h o", o=1).to_broadcast([128, 2, 64]),
                    op=ALU.mult), reads=[B_ro, B_sm], writes=[B_qk16])
                blks = [0, 1, 2, 3, 4] if own else [4]
                for b in blks:
                    T.op("pe", lambda e, b=b: e.transpose(tb6[:, b, :], qk16[:, b * 128:(b + 1) * 128], ident16[:]),
                         reads=[B_qk16, B_ident16], writes=[Bk[6]])
                Bzv = zB(tp, 640, 768)
                if own:
                    T.op("act", lambda e: e.activation(out=QT[:, j, :, :], in_=tb6[:, 0:4, :], func=AF.Identity),
                         reads=[Bk[6]], writes=[B_QT])
                    T.op("act", lambda e: e.activation(out=kvst[:, 0:128], in_=tb6[:, 4, :], func=AF.Identity),
                         reads=[Bk[6]], writes=[B_kvst])
                    T.op("pool", lambda e: e.tensor_copy(kvst[:, 128:258].rearrange("p (g d) -> p g d", g=2)[:, :, 0:64],
                                                         z[:, 640:768].rearrange("p (g d) -> p g d", g=2)),
                         reads=Bzv, writes=[B_kvst])
                    T.dma("sp", lambda e: e.dma_start(out=cc1k_in[:, j * 64:(j + 1) * 64], in_=kvst[:, 0:128].bitcast(F32)),
                          reads=[B_kvst], writes=[B_cc1in])
                    T.dma("sp", lambda e: e.dma_start(out=cc1v_in[j // 8][:, (j % 8) * 65:(j % 8 + 1) * 65], in_=kvst[:, 128:258].bitcast(F32)),
                          reads=[B_kvst], writes=[B_cc1v_in[j // 8]])
                    Bgr = zB(tp, 1824, 2336)
                    T.op("act", lambda e: e.activation(out=sgA[:], in_=z[:, 1824:2336], func=AF.Exp, scale=-1.0), reads=Bgr, writes=[B_sgA])
                    T.op("dve", lambda e: e.tensor_scalar(out=sgA[:], in0=sgA[:], scalar1=1.0, scalar2=None, op0=ALU.add), reads=[B_sgA], writes=[B_sgA])
                    T.op("dve", lambda e: e.reciprocal(out=sgA[:], in_=sgA[:]), reads=[B_sgA], writes=[B_sgA])
                    T.op("pool", lambda e: e.tensor_tensor(out=sgA[:], in0=sgA[:], in1=z[:, 1824:2336], op=ALU.mult),
                         reads=[B_sgA] + Bgr, writes=[B_sgA])
                    T.op("pool", lambda e: e.tensor_tensor(out=sgr16[:], in0=sgA[:], in1=ggla[:], op=ALU.mult),
                         reads=[B_sgA, B_ggla], writes=[B_sgr16])
                    T.dma("sp", lambda e: e.dma_start(out=gla_scr[j, :, 2048:2560], in_=sgr16[:]), reads=[B_sgr16], writes=[B_glascr_a[j]])
                else:
                    T.op("act", lambda e: e.activation(out=KTc[:, t * 128:(t + 1) * 128], in_=tb6[:, 4, :], func=AF.Identity),
                         reads=[Bk[6]], writes=[B_KTc])
                    T.op("pool", lambda e: e.tensor_copy(Vc[:, t, :].rearrange("p (g d) -> p g d", g=2)[:, :, 0:64],
                                                         z[:, 640:768].rearrange("p (g d) -> p g d", g=2)),
                         reads=Bzv, writes=[B_Vc])
                return T

            def build_G(t):
                T = Rec()
                own = t >= 2
                j = t - 2
                tp = t % 2
                z = zr[tp]
                hT, B_hT = hTr[tp]
                Lg, B_Lg = Lgr[tp]
                T.op("pe", lambda e: e.matmul(bank[2][:, 0:256], lhsT=tri[:, 0, :], rhs=Lg[:, 0:256], start=True, stop=True),
                     reads=[B_tri, B_Lg], writes=[Bk[2]])
                T.op("pe", lambda e: e.matmul(bank[2][:, 256:512], lhsT=tri[:, 1, :], rhs=Lg[:, 256:512], start=True, stop=True),
                     reads=[B_tri, B_Lg], writes=[Bk[2]])
                T.op("pe", lambda e: e.matmul(bank[3][:, 0:256], lhsT=tri[:, 2, :], rhs=Lg[:, 0:256], start=True, stop=True),
                     reads=[B_tri, B_Lg], writes=[Bk[3]])
                T.op("pe", lambda e: e.matmul(bank[3][:, 256:512], lhsT=tri[:, 3, :], rhs=Lg[:, 256:512], start=True, stop=True),
                     reads=[B_tri, B_Lg], writes=[Bk[3]])
                for q4 in range(4):
                    T.op("pe", lambda e, q4=q4: e.matmul(bank[4][:, 2 * q4:2 * q4 + 2], lhsT=Lg[:, q4 * 128:(q4 + 1) * 128], rhs=onesf[:, 0:2],
                                                         start=True, stop=True),
                         reads=[B_Lg, B_onesf], writes=[Bk[4]])
                T.op("act", lambda e: e.activation(out=Er[:], in_=bank[3], func=AF.Exp, scale=-1.0 / 16), reads=[Bk[3]], writes=[B_Er])
                T.op("act", lambda e: e.activation(out=dec[:, t, :], in_=bank[4][:, 0:8].rearrange("p (q two) -> p q two", two=2)[:, :, 0],
                                                   func=AF.Exp, scale=-1.0 / 16),
                     reads=[Bk[4]], writes=[B_dec])
                if own:
                    T.op("act", lambda e: e.activation(out=Eq[:], in_=bank[2], func=AF.Exp, scale=-1.0 / 16), reads=[Bk[2]], writes=[B_Eq])
                    T.op("act", lambda e: e.activation(out=Ek[:], in_=bank[2], func=AF.Exp, scale=1.0 / 16), reads=[Bk[2]], writes=[B_Ek])
                gk = z[:, 1024:1280]
                Bgk = zB(tp, 1024, 1280)
                for d_ in range(2):
                    T.op("dve" if d_ == 0 else "pool",
                         lambda e, d_=d_: e.tensor_tensor(out=kup[:, d_ * 256:(d_ + 1) * 256], in0=gk,
                                                          in1=Er[:, d_ * 256:(d_ + 1) * 256], op=ALU.mult),
                         reads=Bgk + [B_Er], writes=[B_kup])
                s_t, B_s = stg[0]
                if own:
                    gq = z[:, 768:1024]
                    for d_ in range(2):
                        T.op("dve", lambda e, d_=d_: e.scalar_tensor_tensor(
                            out=qkin[:, d_, :], in0=gq, scalar=0.125, in1=Eq[:, d_ * 256:(d_ + 1) * 256], op0=ALU.mult, op1=ALU.mult),
                            reads=zB(tp, 768, 1024) + [B_Eq], writes=[B_qkin])
                        T.op("pool", lambda e, d_=d_: e.tensor_tensor(out=qkin[:, 2 + d_, :], in0=gk,
                                                                      in1=Ek[:, d_ * 256:(d_ + 1) * 256], op=ALU.mult),
                             reads=Bgk + [B_Ek], writes=[B_qkin])
                T.op("pool", lambda e: e.tensor_copy(s_t[:, 1536:2048], z[:, 1280:1792]), reads=zB(tp, 1280, 1792), writes=[B_s])
                v16 = s_t[:, 1536:2048]
                updv = psB[:, :].rearrange("p (d j c) -> p d j c", d=2, j=2)
                for d_ in range(2):
                    for jj in range(2):
                        T.op("pe", lambda e, d_=d_, jj=jj: e.matmul(
                            updv[:, d_, jj, :], lhsT=kup[:, d_ * 256 + jj * 128:d_ * 256 + (jj + 1) * 128],
                            rhs=v16[:, jj * 256:(jj + 1) * 256], start=True, stop=True),
                            reads=[B_kup, B_s], writes=[Bk[2 + d_]])
                U_t, B_Ut = Ust[t % 2]
                Uv = U_t[:].rearrange("p d (j c) -> p d j c", j=2)
                for d_ in range(2):
                    T.op("dve", lambda e, d_=d_: e.tensor_copy(Uv[0:64, d_, :, :], updv[0:64, d_, :, 0:128]),
                         reads=[Bk[2 + d_]], writes=[B_Ut])
                    T.op("dve", lambda e, d_=d_: e.tensor_copy(Uv[64:128, d_, :, :], updv[64:128, d_, :, 128:256]),
                         reads=[Bk[2 + d_]], writes=[B_Ut])
                T.dma("sp", lambda e: e.dma_start(out=u_scr[t, :, :], in_=U_t[:].rearrange("p d c -> p (d c)")),
                      reads=[B_Ut], writes=[B_uscr[t]])
                if own:
                    for a_ in range(4):
                        for jj in range(2):
                            T.op("pe", lambda e, a_=a_, jj=jj: e.transpose(tb5[:, a_ * 2 + jj, :], qkin[:, a_, jj * 128:(jj + 1) * 128], ident16[:]),
                                 reads=[B_qkin, B_ident16], writes=[Bk[5]])
                    T.op("act", lambda e: e.activation(out=s_t[:, 1024:1536].rearrange("p (a b) -> p a b", b=128),
                                                       in_=tb5[:, 0:4, :], func=AF.Identity),
                         reads=[Bk[5]], writes=[B_s])
                    T.op("act", lambda e: e.activation(out=hT[:, 0:4, :], in_=tb5[:, 4:8, :], func=AF.Identity),
                         reads=[Bk[5]], writes=[B_hT])
                    attv = psB[:, :].rearrange("p (a b) -> p a b", b=128)
                    for d_ in range(2):
                        for h in range(4):
                            jj, i = h // 2, h % 2
                            T.op("pe", lambda e, d_=d_, jj=jj, i=i: e.matmul(
                                attv[:, i * 4 + d_ * 2 + jj, :], lhsT=hT[i * 64:(i + 1) * 64, d_ * 2 + jj, :],
                                rhs=s_t[i * 64:(i + 1) * 64, 1024 + (d_ * 2 + jj) * 128:1024 + (d_ * 2 + jj + 1) * 128],
                                start=True, stop=True),
                                reads=[B_hT, B_s], writes=[Bk[2 + i]])
                    T.op("dve", lambda e: e.tensor_tensor(out=s_t[:, 0:1024].rearrange("p (a b) -> p a b", b=128),
                                                          in0=attv, in1=mask8[:], op=ALU.mult),
                         reads=[Bk[2], Bk[3], B_mask8], writes=[B_s])
                    T.dma("sp", lambda e: e.dma_start(out=gla_scr[j, :, 0:2048], in_=s_t[:, 0:2048]), reads=[B_s], writes=[B_glascr[j]])
                return T

            def build_C(t):
                T = Rec()
                U_t, B_Ut = Ust[t % 2]
                ctx_ = t < 2
                first = t in (0, 2)
                for d_ in range(2):
                    if ctx_:
                        cu, ca, Bc = Sctx[:, d_, :], Actx[:, :], B_Sctxd[d_]
                    else:
                        cu, ca, Bc = cst[:, 4 + d_ * 256:4 + (d_ + 1) * 256], cst[:, d_ * 2:d_ * 2 + 2], B_cstd[d_]
                    dcol = dec[:, t, d_ * 2:d_ * 2 + 2]
                    if first:
                        T.op("dve", lambda e, cu=cu, d_=d_: e.tensor_copy(cu, U_t[:, d_, :]), reads=[B_Ut], writes=[Bc])
                        if not (ctx_ and d_ == 0):
                            T.op("dve", lambda e, ca=ca, dcol=dcol: e.tensor_copy(ca, dcol), reads=[B_dec], writes=[Bc])
                        continue
                    for jj in range(2):
                        cs = cu[:, jj * 128:(jj + 1) * 128]
                        us = U_t[:, d_, jj * 128:(jj + 1) * 128]
                        if d_ == 0:
                            T.op("dve", lambda e, cs=cs, us=us, jj=jj, dcol=dcol: e.scalar_tensor_tensor(
                                out=cs, in0=cs, scalar=dcol[:, jj:jj + 1], in1=us, op0=ALU.mult, op1=ALU.add),
                                reads=[Bc, B_dec, B_Ut], writes=[Bc])
                        else:
                            T.op("dve", lambda e, cs=cs, us=us, jj=jj, ca=ca: e.scalar_tensor_tensor(
                                out=cs, in0=us, scalar=ca[:, jj:jj + 1], in1=cs, op0=ALU.mult, op1=ALU.add),
                                reads=[Bc, B_Ut], writes=[Bc])
                    if not (ctx_ and d_ == 0):
                        T.op("dve", lambda e, ca=ca, dcol=dcol: e.tensor_tensor(out=ca, in0=ca, in1=dcol, op=ALU.mult),
                             reads=[Bc, B_dec], writes=[Bc])
                return T

            merge_emit(P, [build_FM(0)])
            for t in range(NTA):
                chains, wts = [], []
                if t + 1 < NTA:
                    chains.append(build_FM(t + 1))
                    wts.append(0.5)
                chains.append(build_A(t))
                wts.append(1.0)
                chains.append(build_G(t))
                wts.append(0.75)
                if t >= 1:
                    chains.append(build_C(t - 1))
                    wts.append(1.0)
                merge_emit(P, chains, wts)
            merge_emit(P, [build_C(NTA - 1)])

            z, B_z = zr[(NTA - 1) % 2], B_zg[(NTA - 1) % 2][0]
            hf, B_hf = hfr[0]
            junk, B_junk = hfr[1]
            if debug and debug[0] == "stageA":
                P.dma("sp", lambda e: e.dma_start(out=dbg_d[0, :, 0:INW], in_=z[:]), reads=[B_z], writes=[B_out])
                P.dma("sp", lambda e: e.dma_start(out=dbg_d[1, :, 0:512], in_=Lgr[(NTA - 1) % 2][0][:]), reads=[Lgr[(NTA - 1) % 2][1]], writes=[B_out])
                P.dma("sp", lambda e: e.dma_start(out=dbg_d[2, :, 0:512], in_=Ust[(NTA - 1) % 2][0][:]), reads=[Ust[(NTA - 1) % 2][1]], writes=[B_out])
                P.dma("sp", lambda e: e.dma_start(out=dbg_d[3, :, 0:4], in_=dec[:, NTA - 1, :]), reads=[B_dec], writes=[B_out])
                P.op("dve", lambda e: e.tensor_copy(hf[:, 0:512], QT[:, NT - 1, :, :]), reads=[B_QT], writes=[B_hf])
                P.dma("sp", lambda e: e.dma_start(out=dbg_d[4, :, 0:512], in_=hf[:, 0:512]), reads=[B_hf], writes=[B_out])
                P.op("dve", lambda e: e.tensor_copy(junk[:, 0:258], kvst[:]), reads=[B_kvst], writes=[B_junk])
                P.dma("sp", lambda e: e.dma_start(out=dbg_d[5, :, 0:258], in_=junk[:, 0:258]), reads=[B_junk], writes=[B_out])
                P.barrier()
                P.run(es)
                return nc

            P.barrier()
            sW.close()
            st1.close()
            stQ = ExitStack()
            es.enter_context(stQ)
            Sst, _ = sbuf(stQ, "Sst", [128, NT, 2, 256], BF16)
            B_Sstd = [Buf(), Buf()]
            catg, B_catg = sbuf(stQ, "catg", [128, NT, 512], BF16)
            sA = _sA
            U, _ = sbuf(sA, "U", [128, NTA, 2, 256])
            P.dma("sp", lambda e: e.dma_start(out=U[:].rearrange("p t d c -> p t (d c)"), in_=u_scr.ap().rearrange("t p c -> p t c")),
                  reads=B_uscr, writes=[B_U])
            S2, _ = sbuf(sA, "S2", [128, 2, 256])
            G4, B_G4 = sbuf(sA, "G4", [128, 4, 516])
            tAd = [sbuf(sA, "tA%d" % i, [128, 4]) for i in range(2)]
            tUd = [sbuf(sA, "tU%d" % i, [128, 256]) for i in range(2)]
            B_S2d = [Buf(), Buf()]
            wm2 = [sbuf(sA, "wmb%d" % i, [128, 8, 512]) for i in range(2)]
            bm2 = [sbuf(sA, "bmb%d" % i, [2, 512]) for i in range(2)]
            mch2 = [sbuf(sA, "mchb%d" % i, [2, 512]) for i in range(2)]
            bct2 = [sbuf(sA, "bctb%d" % i, [128, 512]) for i in range(2)]
            gB, B_gB = sbuf(sA, "gB", [128, D])
            P.dma("sp", lambda e: e.dma_start(out=gB[:], in_=g2_in[:, :]), writes=[B_gB])
            ENGd = ["dve", "dve"]

            def step(T, eng, S_ap, BS, t, d_):
                for jj in range(2):
                    T.op(eng, lambda e, jj=jj: e.scalar_tensor_tensor(
                        out=S_ap[:, jj * 128:(jj + 1) * 128], in0=S_ap[:, jj * 128:(jj + 1) * 128],
                        scalar=dec[:, t, d_ * 2 + jj:d_ * 2 + jj + 1], in1=U[:, t, d_, jj * 128:(jj + 1) * 128],
                        op0=ALU.mult, op1=ALU.add), reads=[BS, B_dec, B_U], writes=[BS])

            def chain2(d_):
                T = Rec()
                eng = ENGd[d_]
                tA, B_tA = tAd[d_]
                tU, B_tU = tUd[d_]
                T.op(eng, lambda e: e.tensor_copy(S2[:, d_, :], Sctx[:, d_, :]), reads=[B_Sctxd[d_]], writes=[B_S2d[d_]])
                for i in (range(4) if d_ == 0 else range(3, -1, -1)):
                    fl = flags[:, d_ * 4 + i:d_ * 4 + i + 1]
                    T.op(eng, lambda e, i=i: e.tensor_scalar(out=tA[:, 0:2], in0=G4[:, i, d_ * 2:d_ * 2 + 2],
                                                             scalar1=-1.0, scalar2=None, op0=ALU.add),
                         reads=[B_G4], writes=[B_tA])
                    T.op(eng, lambda e, fl=fl: e.tensor_scalar(out=tA[:, 0:2], in0=tA[:, 0:2], scalar1=fl, scalar2=None, op0=ALU.mult),
                         reads=[B_tA, B_flags], writes=[B_tA])
                    T.op(eng, lambda e: e.tensor_scalar(out=tA[:, 2:4], in0=tA[:, 0:2], scalar1=1.0, scalar2=None, op0=ALU.add),
                         reads=[B_tA], writes=[B_tA])
                    T.op(eng, lambda e, i=i, fl=fl: e.tensor_scalar(out=tU[:], in0=G4[:, i, 4 + d_ * 256:4 + (d_ + 1) * 256],
                                                                    scalar1=fl, scalar2=None, op0=ALU.mult),
                         reads=[B_G4, B_flags], writes=[B_tU])
                    for jj in range(2):
                        T.op(eng, lambda e, jj=jj: e.scalar_tensor_tensor(
                            out=S2[:, d_, jj * 128:(jj + 1) * 128], in0=S2[:, d_, jj * 128:(jj + 1) * 128],
                            scalar=tA[:, 2 + jj:3 + jj], in1=tU[:, jj * 128:(jj + 1) * 128], op0=ALU.mult, op1=ALU.add),
                            reads=[B_S2d[d_], B_tA, B_tU], writes=[B_S2d[d_]])
                for t in (range(2, NTA) if d_ == 0 else range(NTA - 1, 1, -1)):
                    T.op("act", lambda e, t=t: e.activation(out=Sst[:, t - 2, d_, :], in_=S2[:, d_, :], func=AF.Identity),
                         reads=[B_S2d[d_]], writes=[B_Sstd[d_]])
                    step(T, eng, S2[:, d_, :], B_S2d[d_], t, d_)
                return T

            P.dma("sp", lambda e: e.dma_start(out=cc2_in[:, :], in_=cst[:]), reads=B_cstd, writes=[B_cc2in])
            P.dma("pool", lambda e: e.collective_compute("AllGather", ALU.bypass, replica_groups=GROUPS,
                                                         ins=[cc2_in.ap().opt()], outs=[cc2_out.ap().opt()]),
                  reads=[B_cc2in], writes=[B_cc2out], inc=1)
            P.dma("pool", lambda e: e.dma_start(out=G4[:], in_=cc2_out.ap().rearrange("(r p) c -> p r c", p=128)),
                  reads=[B_cc2out], writes=[B_G4])
            P.dma("pool", lambda e: e.collective_compute("AllGather", ALU.bypass, replica_groups=GROUPS,
                                                         ins=[cc1k_in.ap().opt()], outs=[cc1k_out.ap().opt()]),
                  reads=[B_cc1in], writes=[B_cc1out], inc=1)
            for hh in range(2):
                P.dma("pool", lambda e, hh=hh: e.collective_compute("AllGather", ALU.bypass, replica_groups=GROUPS,
                                                                    ins=[cc1v_in[hh].ap().opt()], outs=[cc1v_out[hh].ap().opt()]),
                      reads=[B_cc1v_in[hh]], writes=[B_cc1v_out[hh]], inc=1)
            T_ad1, T_ad2 = Rec(), Rec()
            adaln_groups(T_ad1, range(4, 8), wm2, bm2, mch2, bct2, gB, B_gB, (0, 1, 1))
            adaln_groups(T_ad2, range(8, 12), wm2, bm2, mch2, bct2, gB, B_gB, (0, 1, 1))
            merge_emit(P, [T_ad1])
            merge_emit(P, [chain2(0), chain2(1), T_ad2], [1.0, 1.0, 0.4])
            P.barrier()

        sTp = ExitStack()
        es.enter_context(sTp)
        KT, B_KT = sbuf(sTp, "KT", [128, 66 * 128], BF16)
        VA, B_VA = sbuf(sTp, "VA", [128, 66, 130], BF16)
        wout16, B_wout = sbuf(sTp, "wout16", [128, 8, D], BF16)
        gatt, B_gatt = sbuf(sTp, "gatt", [128, 512])
        gt1, B_gt1 = sbuf(sTp, "gt1", [128, D])
        gs2, B_gs2 = sbuf(sTp, "gs2", [128, D])
        sh2, B_sh2 = sbuf(sTp, "sh2", [128, D])
        xr = [sbuf(sTp, "xr%d" % i, [128, D]) for i in range(2)]
        P.dma("sp", lambda e: e.dma_start(out=gs2[:], in_=bc_scr[2, :, :]), reads=[B_bcscr[2]], writes=[B_gs2])
        P.dma("sp", lambda e: e.dma_start(out=sh2[:], in_=bc_scr[1, :, :]), reads=[B_bcscr[1]], writes=[B_sh2])
        P.dma("sp", lambda e: e.dma_start(out=gt1[:], in_=bc_scr[0, :, :]), reads=[B_bcscr[0]], writes=[B_gt1])
        for k in range(8):
            xs_t, B_xs = xr[k % 2]
            P.dma("sp", lambda e, k=k, xs_t=xs_t: e.dma_start(out=xs_t[:], in_=wout_in[:, k, :]), writes=[B_xs])
            P.op("dve", lambda e, k=k, xs_t=xs_t: e.tensor_tensor(out=wout16[:, k, :], in0=xs_t[:], in1=gt1[:], op=ALU.mult),
                 reads=[B_xs, B_gt1], writes=[B_wout])
        P.dma("sp", lambda e: e.dma_start(out=gatt[:], in_=gatt_in[:, :]), writes=[B_gatt])
        P.op("pool", lambda e: e.tensor_copy(KT[:, 0:256], KTc[:]), reads=[B_KTc], writes=[B_KT])
        P.op("pool", lambda e: e.tensor_copy(VA[:, 0:2, :], Vc[:]), reads=[B_Vc], writes=[B_VA])
        with ExitStack() as sG:
            NC2 = 4
            ld = [sbuf(sG, "ld%d" % i, [128, 2560], BF16) for i in range(NC2)]
            osqs = [sbuf(sG, "osq%d" % i, [128, 512]) for i in range(NC2)]
            ons = [sbuf(sG, "on%d" % i, [128, 512]) for i in range(NC2)]
            gss = [sbuf(sG, "gsm%d" % i, [128, 16]) for i in range(NC2)]
            B_catgt = [Buf() for _ in range(NT)]

            def build_P2(c):
                T = Rec()
                l_t, B_l = ld[c]
                osq, B_osq = osqs[c]
                on, B_on = ons[c]
                gs_, B_gs = gss[c]
                for j in range(c, NT, NC2):
                    T.dma("sp", lambda e, j=j: e.dma_start(out=l_t[:], in_=gla_scr[j, :, :]), reads=[B_glascr[j], B_glascr_a[j]], writes=[B_l])
                    for h in range(4):
                        jj, i = h // 2, h % 2
                        pb = 2 * c + i
                        oh = bank[pb][:, jj * 128:(jj + 1) * 128]
                        vh = l_t[:, 1536 + h * 128:1536 + (h + 1) * 128]
                        bf_ = i * 4 + jj
                        bb_ = i * 4 + 2 + jj
                        T.op("pe", lambda e, bf_=bf_, oh=oh, vh=vh: e.matmul(oh, lhsT=l_t[:, bf_ * 128:(bf_ + 1) * 128], rhs=vh, start=True, stop=False),
                             reads=[B_l], writes=[Bk[pb]])
                        T.op("pe", lambda e, bb_=bb_, oh=oh, vh=vh: e.matmul(oh, lhsT=l_t[:, bb_ * 128:(bb_ + 1) * 128], rhs=vh, start=False, stop=False),
                             reads=[B_l], writes=[Bk[pb]])
                        for d_ in range(2):
                            T.op("pe", lambda e, j=j, jj=jj, i=i, d_=d_, oh=oh: e.matmul(
                                oh, lhsT=l_t[i * 64:(i + 1) * 64, 1024 + (d_ * 2 + jj) * 128:1024 + (d_ * 2 + jj + 1) * 128],
                                rhs=Sst[i * 64:(i + 1) * 64, j, d_, jj * 128:(jj + 1) * 128], start=False, stop=(d_ == 1)),
                                reads=[B_l, B_Sstd[d_]], writes=[Bk[pb]])
                    for i in range(2):
                        pb = 2 * c + i
                        T.op("act", lambda e, i=i, pb=pb: e.activation(out=on[:].rearrange("p (jj i c) -> p i jj c", i=2, c=128)[:, i, :, :],
                                                                       in_=bank[pb][:, 0:256].rearrange("p (jj c) -> p jj c", c=128), func=AF.Identity),
                             reads=[Bk[pb]], writes=[B_on])
                    T.op("act", lambda e: e.activation(out=osq[:], in_=on[:], func=AF.Square), reads=[B_on], writes=[B_osq])
                    T.op("dve", lambda e: e.tensor_reduce(out=gs_[:, 0:4], in_=osq[:].rearrange("p (h d) -> p h d", d=128), axis=AX.X, op=ALU.add),
                         reads=[B_osq], writes=[B_gs])
                    rstd_from_ss(gs_[:, 0:4], gs_[:, 8:12], 128, B_gs, B_gs, gs_[:, 4:8], B_gs, T=T)
                    T.op("dve", lambda e: e.tensor_tensor(out=osq[:].rearrange("p (h d) -> p h d", d=128),
                                                          in0=on[:].rearrange("p (h d) -> p h d", d=128),
                                                          in1=gs_[:, 8:12].rearrange("p (h o) -> p h o", o=1).to_broadcast([128, 4, 128]),
                                                          op=ALU.mult), reads=[B_on, B_gs], writes=[B_osq])
                    T.op("dve" if (j // NC2) % 2 == 0 else "pool", lambda e, j=j: e.tensor_tensor(out=catg[:, j, :], in0=osq[:], in1=l_t[:, 2048:2560], op=ALU.mult),
                         reads=[B_osq, B_l], writes=[B_catgt[j]])
                return T

            merge_emit(P, [build_P2(c_) for c_ in range(NC2)])
            for r in range(4):
                P.dma("sp", lambda e, r=r: e.dma_start(out=KT[:, 256 + r * 2048:256 + (r + 1) * 2048].bitcast(F32),
                                                       in_=cc1k_out[r * 128:(r + 1) * 128, :]),
                      reads=[B_cc1out], writes=[B_KT])
                for hh in range(2):
                    t0_ = 2 + r * NT + hh * 8
                    P.dma("sp", lambda e, r=r, hh=hh, t0_=t0_: e.dma_start(
                        out=VA[:, t0_:t0_ + 8, :].rearrange("p t c -> p (t c)").bitcast(F32),
                        in_=cc1v_out[hh][r * 128:(r + 1) * 128, :]),
                        reads=[B_cc1v_out[hh]], writes=[B_VA])
            P.barrier()

        if debug and debug[0] == "gla":
            with ExitStack() as sD:
                dt_, B_dt = sbuf(sD, "dt_", [128, NT, 512])
                P.op("dve", lambda e: e.tensor_copy(dt_[:], catg[:]), reads=B_catgt, writes=[B_dt])
                P.dma("sp", lambda e: e.dma_start(out=dbg_d.rearrange("j p c -> p j c"), in_=dt_[:]), reads=[B_dt], writes=[B_out])
                P.barrier()
            P.run(es)
            return nc

        with ExitStack() as sT:
            PT = [sbuf(sT, "PT%d" % i, [128, 1024], BF16) for i in range(3)]
            oT, B_oT = sbuf(sT, "oT", [65, 2, 512])
            ao, B_ao = sbuf(sT, "ao", [128, 512])
            aj, B_aj = sbuf(sT, "aj", [128, 512])
            asm, B_asm = sbuf(sT, "asm", [128, 16])
            ca16, B_ca16 = sbuf(sT, "ca16", [128, 512], BF16)
            catT, B_catT = sbuf(sT, "catT", [128, 8, 128], BF16)
            x1t = [sbuf(sT, "x1t%d" % i, [128, D]) for i in range(2)]
            hfE, B_hfE = sbuf(sT, "hfE", [128, D])
            h16E, B_h16E = sbuf(sT, "h16E", [128, D], BF16)
            smE, B_smE = sbuf(sT, "smE", [128, 4])
            wkE = (hfE, B_hfE, smE[:, 0:1], B_smE, smE[:, 1:2], B_smE, smE[:, 2:3], B_smE, hfE, B_hfE, h16E, B_h16E)
            psS = [psA, psB]
            BS_ = [[Bk[0], Bk[1]], [Bk[2], Bk[3]]]
            def emit_qk(T, qb, kt):
                bi = kt % 2
                for g in range(2):
                    T.op("pe", lambda e, bi=bi, g=g, kt=kt, qb=qb: e.matmul(
                        psS[bi][:, g * 512:(g + 1) * 512], lhsT=KT[g * 64:(g + 1) * 64, kt * 128:(kt + 1) * 128],
                        rhs=QT[g * 64:(g + 1) * 64, qb, :, :], start=True, stop=True),
                        reads=[B_KT, B_QT], writes=[BS_[bi][g]])

            def build_main(qb):
                T = Rec()
                for kt in range(66):
                    bi = kt % 2
                    p_t, B_p = PT[kt % 3]
                    T.op("act", lambda e, bi=bi, p_t=p_t: e.activation(out=p_t[:], in_=psS[bi][:, :], func=AF.Exp, scale=0.125),
                         reads=BS_[bi], writes=[B_p])
                    if kt + 2 < 66:
                        emit_qk(T, qb, kt + 2)
                    elif qb + 1 < NT:
                        emit_qk(T, qb + 1, kt + 2 - 66)
                    for g in range(2):
                        T.op("pe", lambda e, g=g, kt=kt, p_t=p_t: e.matmul(
                            bank[4 + g][0:65, :], lhsT=VA[:, kt, g * 65:(g + 1) * 65], rhs=p_t[:, g * 512:(g + 1) * 512],
                            start=(kt == 0), stop=(kt == 65)),
                            reads=[B_VA, B_p], writes=[Bk[4 + g]])
                return T

            def build_E(qb):
                T = Rec()
                for g in range(2):
                    for i in range(4):
                        T.op("pe", lambda e, g=g, i=i: e.matmul(bank[6 + g][:, i * 128:i * 128 + 65], lhsT=oT[:, g, i * 128:(i + 1) * 128],
                                                               rhs=identf[0:65, 0:65], start=True, stop=True),
                             reads=[B_oT, B_identf], writes=[Bk[6 + g]])
                for g in range(2):
                    ov = bank[6 + g].rearrange("p (i c) -> p i c", c=128)
                    T.op("dve", lambda e, g=g, ov=ov: e.reciprocal(out=asm[:, g * 4:(g + 1) * 4].rearrange("p (i o) -> p i o", o=1), in_=ov[:, :, 64:65]),
                         reads=[Bk[6 + g]], writes=[B_asm])
                    T.op("dve", lambda e, g=g, ov=ov: e.tensor_tensor(
                        out=ao[:, g * 256:(g + 1) * 256].rearrange("p (i d) -> p i d", d=64), in0=ov[:, :, 0:64],
                        in1=asm[:, g * 4:(g + 1) * 4].rearrange("p (i o) -> p i o", o=1).to_broadcast([128, 4, 64]), op=ALU.mult),
                        reads=[Bk[6 + g], B_asm], writes=[B_ao])
                T.op("act", lambda e: e.activation(out=aj[:], in_=ao[:], func=AF.Square, accum_out=asm[:, 8:9]),
                     reads=[B_ao], writes=[B_aj, B_asm])
                rstd_from_ss(asm[:, 8:9], asm[:, 10:11], 512, B_asm, B_asm, asm[:, 9:10], B_asm, T=T)
                T.op("dve", lambda e: e.scalar_tensor_tensor(out=ca16[:], in0=ao[:], scalar=asm[:, 10:11], in1=gatt[:],
                                                             op0=ALU.mult, op1=ALU.mult),
                     reads=[B_ao, B_asm, B_gatt], writes=[B_ca16])
                for k in range(8):
                    src = ca16[:, k * 128:(k + 1) * 128] if k < 4 else catg[:, qb, (k - 4) * 128:(k - 3) * 128]
                    T.op("pe", lambda e, k=k, src=src: e.transpose(tb6[:, k, :], src, ident16[:]),
                         reads=[B_ca16, B_catgt[qb], B_ident16], writes=[Bk[6]])
                T.op("dve", lambda e: e.tensor_copy(catT[:], tb6), reads=[Bk[6]], writes=[B_catT])
                x_t, B_x = xr[qb % 2]
                x1_t, B_x1 = x1t[qb % 2]
                T.dma("sp", lambda e: e.dma_start(out=x_t[:], in_=x_in[qb, :, :]), writes=[B_x])
                for nh_ in range(2):
                    for k in range(8):
                        T.op("pe", lambda e, nh_=nh_, k=k: e.matmul(bank[7], lhsT=catT[:, k, :], rhs=wout16[:, k, nh_ * 512:(nh_ + 1) * 512],
                                                                    start=(k == 0), stop=(k == 7)),
                             reads=[B_catT, B_wout], writes=[Bk[7]])
                    hs = slice(nh_ * 512, (nh_ + 1) * 512)
                    T.op("dve", lambda e, hs=hs: e.tensor_tensor(out=x1_t[:, hs], in0=bank[7], in1=x_t[:, hs], op=ALU.add),
                         reads=[Bk[7], B_x], writes=[B_x1])
                T.dma("sp", lambda e: e.dma_start(out=x1_scr[qb, :, :], in_=x1_t[:]), reads=[B_x1], writes=[B_x1scr[qb]])
                norm_mod_transpose(x1_t[:], B_x1, gs2, B_gs2, sh2, B_sh2,
                                   h2T[:, :, 1 + qb * 128:1 + (qb + 1) * 128], B_h2T, wkE, T=T, cp_eng="dve")
                if qb == 0:
                    T.dma("sp", lambda e: e.dma_start(out=cc3_in[0:1, :], in_=h16E[0:1, :]), reads=[B_h16E], writes=[B_cc3in])
                if qb == NT - 1:
                    T.dma("sp", lambda e: e.dma_start(out=cc3_in[1:2, :], in_=h16E[127:128, :]), reads=[B_h16E], writes=[B_cc3in])
                return T

            emit_qk(P, 0, 0)
            emit_qk(P, 0, 1)
            merge_emit(P, [build_main(0)])
            for qb in range(NT):
                for g in range(2):
                    P.op("dve", lambda e, g=g: e.tensor_copy(oT[:, g, :], bank[4 + g][0:65, :]),
                         reads=[Bk[4 + g]], writes=[B_oT])
                if qb + 1 < NT:
                    merge_emit_staged(P, build_main(qb + 1), build_E(qb))
                else:
                    merge_emit(P, [build_E(qb)])
            P.barrier()
        sTp.close()
        stQ.close()
        sU.close()
        stP.close()

        if debug and debug[0] == "x1":
            with ExitStack() as sD:
                dt_, B_dt = sbuf(sD, "dt_", [128, NT, D])
                P.dma("sp", lambda e: e.dma_start(out=dt_[:], in_=x1_scr.ap().rearrange("j p c -> p j c")), reads=B_x1scr, writes=[B_dt])
                P.dma("sp", lambda e: e.dma_start(out=dbg_d.rearrange("j p c -> p j c"), in_=dt_[:]), reads=[B_dt], writes=[B_out])
                P.barrier()
            P.run(es)
            return nc

        with ExitStack() as sF:
            x1, B_x1r = sbuf(sF, "x1", [128, NT, D])
            B_x1tile = [Buf() for _ in range(NT)]
            gt2, B_gt2 = sbuf(sF, "gt2", [128, D])
            gfb, B_gfb = sbuf(sF, "gfb", [128, D])
            cw, B_cw = sbuf(sF, "cw", [128, 2 * NPAIR, 3])
            cbt, B_cbt = sbuf(sF, "cbt", [128, 2 * NPAIR])
            sel8, B_sel8 = sbuf(sF, "sel8", [8, 2])
            sel816, B_sel816 = sbuf(sF, "sel816", [8, 2], BF16)
            rows8, B_rows8 = sbuf(sF, "rows8", [8, D], BF16)
            hf, B_hf = sbuf(sF, "hfF", [128, D])
            junk, B_junk = hf, B_hf
            fsm, B_fsm = sbuf(sF, "fsm", [128, 8])
            P.dma("sp", lambda e: e.dma_start(out=gt2[:], in_=bc_scr[3, :, :]), reads=[B_bcscr[3]], writes=[B_gt2])
            P.dma("sp", lambda e: e.dma_start(out=cw[:], in_=cw_in[:, :, :]), writes=[B_cw])
            P.dma("sp", lambda e: e.dma_start(out=cbt[:], in_=cb_in[:, :]), writes=[B_cbt])
            P.dma("sp", lambda e: e.dma_start(out=sel8[:], in_=sel8_in[:, :]), writes=[B_sel8])
            P.op("dve", lambda e: e.tensor_copy(sel816[:], sel8[:]), reads=[B_sel8], writes=[B_sel816])
            for j in range(NT):
                P.dma("sp", lambda e, j=j: e.dma_start(out=x1[:, j, :], in_=x1_scr[j, :, :]), reads=[B_x1scr[j]], writes=[B_x1tile[j]])
            P.dma("sp", lambda e: e.dma_start(out=gfb[:], in_=gf_in[:, :]), writes=[B_gfb])
            P.dma("pool", lambda e: e.collective_compute("AllGather", ALU.bypass, replica_groups=GROUPS,
                                                         ins=[cc3_in.ap().opt()], outs=[cc3_out.ap().opt()]),
                  reads=[B_cc3in], writes=[B_cc3out], inc=1)
            P.dma("sp", lambda e: e.dma_start(out=rows8[:], in_=cc3_out[:, :]), reads=[B_cc3out], writes=[B_rows8])
            for k in range(8):
                P.op("pe", lambda e, k=k: e.matmul(bank[7][:, 2 * k:2 * k + 2], lhsT=rows8[:, k * 128:(k + 1) * 128], rhs=sel816[:],
                                                   start=True, stop=True), reads=[B_rows8, B_sel816], writes=[Bk[7]])
            hv = bank[7][:, 0:16].rearrange("p (k c) -> p k c", c=2)
            P.op("act", lambda e: e.activation(out=h2T[:, :, 0:1], in_=hv[:, :, 0:1], func=AF.Identity), reads=[Bk[7]], writes=[B_h2T])
            P.op("act", lambda e: e.activation(out=h2T[:, :, 2049:2050], in_=hv[:, :, 1:2], func=AF.Identity), reads=[Bk[7]], writes=[B_h2T])

            wu = [sbuf(sF, "wu%d" % i, [128, 8, 256], BF16) for i in range(2)]
            wst = [sbuf(sF, "wst%d" % i, [128, 2048]) for i in range(2)]
            nst = [0]
            wd = [sbuf(sF, "wd%d" % i, [128, 6, D], BF16) for i in range(2)]
            actT, B_actT = sbuf(sF, "actT", [128, 6, 2048], BF16)
            cva = [sbuf(sF, "cva%d" % i, [128, 512]) for i in range(2)]
            cga = [sbuf(sF, "cga%d" % i, [128, 512]) for i in range(2)]
            ot = [(hf, B_hf)] * 2
            psUV = [psA, psB]
            nit = 0

            def final_norm(j):
                o_t, B_o = ot[j % 2]
                P.op("act", lambda e: e.activation(out=junk[:], in_=x1[:, j, :], func=AF.Square, accum_out=fsm[:, 4:5]),
                     reads=[B_x1tile[j]], writes=[B_junk, B_fsm])
                rstd_from_ss(fsm[:, 4:5], fsm[:, 6:7], D, B_fsm, B_fsm, fsm[:, 5:6], B_fsm)
                P.op("dve", lambda e: e.scalar_tensor_tensor(out=o_t[:], in0=x1[:, j, :], scalar=fsm[:, 6:7], in1=gfb[:],
                                                             op0=ALU.mult, op1=ALU.mult),
                     reads=[B_x1tile[j], B_fsm, B_gfb], writes=[B_o])
                P.dma("sp", lambda e: e.dma_start(out=out_d[j, :, :], in_=o_t[:]), reads=[B_o], writes=[B_outs[j]])

            def load_wu(pj):
                wu_t, B_wu = wu[pj % 2]
                ws_t, B_ws = wst[nst[0] % 2]
                nst[0] += 1
                P.dma("sp", lambda e: e.dma_start(out=ws_t[:].rearrange("p (k c) -> p k c", c=256), in_=wup_in[pj, :, :, :]),
                      writes=[B_ws])
                P.op("act", lambda e: e.activation(out=wu_t[:], in_=ws_t[:].rearrange("p (k c) -> p k c", c=256), func=AF.Identity),
                     reads=[B_ws], writes=[B_wu])

            for si, (p0, pn) in enumerate(SLABS):
                wd_t, B_wd = wd[si % 2]
                for jj in range(0, pn, 2):
                    nj = min(2, pn - jj)
                    ws_t, B_ws = wst[nst[0] % 2]
                    nst[0] += 1
                    P.dma("sp", lambda e, ws_t=ws_t, jj=jj, p0=p0, nj=nj: e.dma_start(
                        out=ws_t[:, 0:nj * D].rearrange("p (a c) -> p a c", c=D),
                        in_=wdn_in[p0 + jj:p0 + jj + nj, :, :].rearrange("a p c -> p a c")), writes=[B_ws])
                    for a_ in range(nj):
                        P.op("pool", lambda e, ws_t=ws_t, wd_t=wd_t, jj=jj, a_=a_: e.tensor_tensor(
                            out=wd_t[:, jj + a_, :], in0=ws_t[:, a_ * D:(a_ + 1) * D], in1=gt2[:], op=ALU.mult),
                            reads=[B_ws, B_gt2], writes=[B_wd])
                for jj in range(pn):
                    pj = p0 + jj
                    wu_t, B_wu = wu[pj % 2]
                    if pj == 0:
                        load_wu(0)
                    if pj + 1 < NPAIR:
                        load_wu(pj + 1)
                    for (c0, n) in [TG[1], TG[2], TG[3], TG[0], TG[4]]:
                        bi = nit % 2
                        nit += 1
                        pv = psUV[bi][:, 0:n]
                        pg = psUV[bi][:, 512:512 + n]
                        Bpv, Bpg = Bk[2 * bi], Bk[2 * bi + 1]
                        for k in range(8):
                            P.op("pe", lambda e, pv=pv, wu_t=wu_t, k=k, c0=c0, n=n: e.matmul(pv, lhsT=wu_t[:, k, 0:128], rhs=h2T[:, k, c0:c0 + n],
                                                                                         start=(k == 0), stop=(k == 7)),
                                 reads=[B_wu, B_h2T], writes=[Bpv])
                        for k in range(8):
                            P.op("pe", lambda e, pg=pg, wu_t=wu_t, k=k, c0=c0, n=n: e.matmul(pg, lhsT=wu_t[:, k, 128:256], rhs=h2T[:, k, c0:c0 + n],
                                                                                         start=(k == 0), stop=(k == 7)),
                                 reads=[B_wu, B_h2T], writes=[Bpg])
                        m = n - 2
                        (va, Bva), (ga, Bga) = cva[bi], cga[bi]
                        for (pp, Bpp, ch, a_, Ba) in ((pv, Bpv, pj, va, Bva), (pg, Bpg, NPAIR + pj, ga, Bga)):
                            P.op("act", lambda e, pp=pp, ch=ch, a_=a_, m=m: e.activation(out=a_[:, 0:m], in_=pp[:, 1:1 + m], func=AF.Identity,
                                                                                       bias=cbt[:, ch:ch + 1], scale=cw[:, ch, 1:2]),
                                 reads=[Bpp, B_cbt, B_cw], writes=[Ba])
                            P.op("dve", lambda e, pp=pp, ch=ch, a_=a_, m=m: e.scalar_tensor_tensor(
                                out=a_[:, 0:m], in0=pp[:, 0:m], scalar=cw[:, ch, 0:1], in1=a_[:, 0:m], op0=ALU.mult, op1=ALU.add),
                                reads=[Bpp, B_cw, Ba], writes=[Ba])
                            P.op("dve", lambda e, pp=pp, ch=ch, a_=a_, m=m: e.scalar_tensor_tensor(
                                out=a_[:, 0:m], in0=pp[:, 2:2 + m], scalar=cw[:, ch, 2:3], in1=a_[:, 0:m], op0=ALU.mult, op1=ALU.add),
                                reads=[Bpp, B_cw, Ba], writes=[Ba])
                        P.op("act", lambda e, ga=ga, m=m: e.activation(out=ga[:, 0:m], in_=ga[:, 0:m], func=AF.Silu), reads=[Bga], writes=[Bga])
                        P.op("pool", lambda e, ga=ga, va=va, jj=jj, c0=c0, m=m: e.tensor_tensor(out=actT[:, jj, c0:c0 + m], in0=ga[:, 0:m], in1=va[:, 0:m], op=ALU.mult),
                             reads=[Bga, Bva], writes=[B_actT])
                for j in range(NT):
                    for nh_ in range(2):
                        hs = slice(nh_ * 512, (nh_ + 1) * 512)
                        pb = 4 + (2 * j + nh_) % 2
                        for jj in range(pn):
                            P.op("pe", lambda e, pb=pb, jj=jj, j=j, hs=hs, wd_t=wd_t, pn=pn: e.matmul(
                                bank[pb], lhsT=actT[:, jj, j * 128:(j + 1) * 128], rhs=wd_t[:, jj, hs], start=(jj == 0), stop=(jj == pn - 1)),
                                reads=[B_actT, B_wd], writes=[Bk[pb]])
                        P.op("dve", lambda e, pb=pb, j=j, hs=hs: e.tensor_tensor(out=x1[:, j, hs], in0=bank[pb], in1=x1[:, j, hs], op=ALU.add),
                             reads=[Bk[pb], B_x1tile[j]], writes=[B_x1tile[j]])
                    if si == len(SLABS) - 1:
                        final_norm(j)
            P.barrier()
        P.run(es)
    return nc


def _rope_tables(tok0):
    t = np.arange(tok0, tok0 + 2048, dtype=np.float32)
    row = np.floor(t / 64).astype(np.float32)
    col = (t - row * 64).astype(np.float32)
    freqs = (np.float32(10000.0) ** (-np.arange(0, 32, 2, dtype=np.float32) / np.float32(32))).astype(np.float32)
    ang = np.concatenate([row[:, None] * freqs, col[:, None] * freqs], axis=-1).astype(np.float32)
    return np.cos(ang).astype(np.float32), np.sin(ang).astype(np.float32)


def make_in_maps(x, c, ctx, c_ctx, w_mod, b_mod, g_norm1, w_in, g_q, g_k, w_gate_fwd, b_gate_fwd,
                 w_gate_bwd, b_gate_bwd, g_att_out, g_gla_out, w_out, g_norm2, w_up, conv_w, conv_b,
                 w_down, g_final):
    f = np.float32
    A = lambda a: np.ascontiguousarray(np.asarray(a, dtype=f))
    x, c, ctx, c_ctx = A(x), A(c), A(ctx), A(c_ctx)
    w_mod, b_mod, w_in, w_out, w_up, w_down = A(w_mod)[0], A(b_mod)[0], A(w_in)[0], A(w_out)[0], A(w_up)[0], A(w_down)[0]
    bc = lambda v, n=128: np.ascontiguousarray(np.broadcast_to(np.asarray(v, f)[None, :], (n, v.shape[-1])))
    shared = {}
    shared["w_mod_r"] = A(w_mod.reshape(8, 128, 12, 512).transpose(2, 1, 0, 3))
    shared["b_mod2"] = bc(b_mod, 2)
    shared["g1_bc"] = bc(A(g_norm1)[0])
    shared["g2_bc"] = bc(A(g_norm2)[0])
    shared["gf_bc"] = bc(A(g_final))
    shared["w_in_r"] = A(w_in.reshape(8, 128, INW).transpose(1, 0, 2))
    gq, gk = A(g_q)[0], A(g_k)[0]
    shared["gqk_bc"] = bc(np.concatenate([np.tile(gq, 8), np.tile(gk, 2)]))
    wg = np.zeros((33, 512), f)
    wg[0:16, 0:256] = A(w_gate_fwd)[0]
    wg[16:32, 256:512] = A(w_gate_bwd)[0]
    wg[32, 0:256] = A(b_gate_fwd)[0]
    wg[32, 256:512] = A(b_gate_bwd)[0]
    shared["w_gate_aug"] = wg
    shared["gatt_bc"] = bc(A(g_att_out)[0])
    shared["ggla_bc"] = bc(np.tile(A(g_gla_out)[0], 4))
    shared["w_out_r"] = A(w_out.reshape(8, 128, D).transpose(1, 0, 2))
    wu = w_up.reshape(8, 128, 2, NPAIR, 128)
    shared["w_up_r"] = A(wu.transpose(3, 1, 0, 2, 4).reshape(NPAIR, 128, 8, 256))
    shared["w_down_r"] = A(w_down.reshape(NPAIR, 128, D))
    cw = A(conv_w)[0]
    shared["conv_wT"] = A(cw.reshape(3, 2 * NPAIR, 128).transpose(2, 1, 0))
    shared["conv_bT"] = A(A(conv_b)[0].reshape(2 * NPAIR, 128).T)
    shared["identf"] = np.eye(128, dtype=f)
    s = np.arange(128)[:, None]
    t = np.arange(128)[None, :]
    tri = np.stack([(s <= t), (s >= t), (s > t), (s < t)], axis=1).astype(f)
    shared["tri4"] = A(tri)
    shared["mask8"] = A(np.stack([tri[:, 0], tri[:, 0], tri[:, 1], tri[:, 1], tri[:, 0], tri[:, 0], tri[:, 1], tri[:, 1]], axis=1))
    sel2 = np.zeros((2, 256), f)
    sel2[0, 0:128] = 1
    sel2[1, 128:256] = 1
    shared["sel2"] = sel2
    maps = []
    for core in range(8):
        b, r = core // 4, core % 4
        m = dict(shared)
        m["x_own"] = A(x[b, r * 2048:(r + 1) * 2048].reshape(NT, 128, D))
        m["ctx_b"] = A(ctx[b].reshape(2, 128, D))
        m["cT"] = A(np.stack([c[b], c_ctx], axis=0).reshape(2, 8, 128).transpose(2, 1, 0))
        cs, sn = _rope_tables(r * 2048)
        cs = np.concatenate([np.ones((256, 32), f), cs], axis=0).reshape(NTA, 128, 32).transpose(1, 0, 2)
        sn = np.concatenate([np.zeros((256, 32), f), sn], axis=0).reshape(NTA, 128, 32).transpose(1, 0, 2)
        m["rope_cos"] = A(cs)
        m["rope_sin"] = A(sn)
        fl = np.zeros((128, 8), f)
        for i in range(4):
            fl[:, i] = 1.0 if i < r else 0.0
            fl[:, 4 + i] = 1.0 if i > r else 0.0
        m["flags"] = fl
        s8 = np.zeros((8, 2), f)
        if r > 0:
            s8[(r - 1) * 2 + 1, 0] = 1.0
        if r < 3:
            s8[(r + 1) * 2 + 0, 1] = 1.0
        m["sel8"] = s8
        maps.append(m)
    return maps


_NC_CACHE = {}


def kernel(**inputs):
    if "nc" not in _NC_CACHE:
        _NC_CACHE["nc"] = build()
    nc = _NC_CACHE["nc"]
    maps = make_in_maps(**inputs)
    res = run_bass_kernel_spmd(nc, maps, core_ids=list(range(8)))
    out = np.zeros((2, 8192, D), np.float32)
    for core in range(8):
        b, r = core // 4, core % 4
        out[b, r * 2048:(r + 1) * 2048] = np.asarray(res.results[core]["out"]).reshape(2048, D)
    return out
```
